# Optimizing a Trainium2 kernel written in Bass

```python
import math
import jax, jax.numpy as jnp
from jax import lax
import numpy as np

D_MODEL = 1024
BATCH = 8
SEQ = 4096
DEPTH = 1

CHUNK = 64
PLE_DIM = 256
MIX_WIDTH = D_MODEL
SSM_WIDTH = MIX_WIDTH // 2
POOL_WIDTH = MIX_WIDTH - SSM_WIDTH
SSM_GROUP = 16
SSM_GROUPS = SSM_WIDTH // SSM_GROUP
SSM_STATE = 64
POOL_WINDOWS = (2, 4, 8, 16)
POOL_GROUPS = len(POOL_WINDOWS)
POOL_GROUP_WIDTH = POOL_WIDTH // POOL_GROUPS
POOL_PAD = max(POOL_WINDOWS)
D_FF = 2816
FFN_RES = 0.5
EPS = 1e-6
DT_MIN = 1e-3
DT_MAX = 1e-1

kernel_name = 'hymba_s5_pool_macaron_block'


def rmsnorm(x, g):
    xf = x.astype(jnp.float32)
    y = xf * lax.rsqrt(jnp.mean(xf * xf, axis=-1, keepdims=True) + EPS)
    return (y * g.astype(jnp.float32)).astype(x.dtype)


def swiglu(x, w1, w3, w2):
    return (jax.nn.silu(x @ w1) * (x @ w3)) @ w2


def _complex_linear_combine(e1, e2):
    a1r, a1i, b1r, b1i = e1
    a2r, a2i, b2r, b2i = e2
    return (a2r * a1r - a2i * a1i,
            a2r * a1i + a2i * a1r,
            a2r * b1r - a2i * b1i + b2r,
            a2r * b1i + a2i * b1r + b2i)


def s5_mixer(u, a_re, a_im, log_dt, b_re, b_im, c_re, c_im, d_skip, w_glu, b_glu):
    f32 = jnp.float32
    bsz, seqlen, _ = u.shape
    uf = u.astype(f32)
    ug = uf.reshape(bsz, seqlen, SSM_GROUPS, SSM_GROUP)
    dt = jnp.exp(log_dt.astype(f32))[:, None]
    lam_r = a_re.astype(f32)
    lam_i = a_im.astype(f32)
    mag = jnp.exp(lam_r * dt)
    abar_r = mag * jnp.cos(lam_i * dt)
    abar_i = mag * jnp.sin(lam_i * dt)
    den = lam_r * lam_r + lam_i * lam_i
    nr = abar_r - 1.0
    f_r = (nr * lam_r + abar_i * lam_i) / den
    f_i = (abar_i * lam_r - nr * lam_i) / den
    br = b_re.astype(f32)
    bi = b_im.astype(f32)
    bbar_r = f_r[..., None] * br - f_i[..., None] * bi
    bbar_i = f_r[..., None] * bi + f_i[..., None] * br
    bu_r = jnp.einsum('blgp,gnp->blgn', ug, bbar_r)
    bu_i = jnp.einsum('blgp,gnp->blgn', ug, bbar_i)
    ones_t = jnp.ones((1, seqlen, 1, 1), f32)
    a_seq_r = abar_r[None, None] * ones_t
    a_seq_i = abar_i[None, None] * ones_t
    _, _, s_r, s_i = lax.associative_scan(
        _complex_linear_combine, (a_seq_r, a_seq_i, bu_r, bu_i), axis=1)
    y = (jnp.einsum('gpn,blgn->blgp', c_re.astype(f32), s_r)
         - jnp.einsum('gpn,blgn->blgp', c_im.astype(f32), s_i))
    y = y.reshape(bsz, seqlen, SSM_WIDTH) + d_skip.astype(f32) * uf
    y = jax.nn.gelu(y)
    y = y * jax.nn.sigmoid(y @ w_glu.astype(f32) + b_glu.astype(f32))
    return y.astype(u.dtype)


def pool_mixer(v, w_pool, pool_scale):
    f32 = jnp.float32
    bsz, seqlen, _ = v.shape
    vg = v.astype(f32).reshape(bsz, seqlen, POOL_GROUPS, POOL_GROUP_WIDTH)
    csum = jnp.cumsum(jnp.pad(vg, ((0, 0), (POOL_PAD, 0), (0, 0), (0, 0))), axis=1)
    t = jnp.arange(seqlen)
    outs = []
    for gi, w in enumerate(POOL_WINDOWS):
        win_sum = csum[:, POOL_PAD:, gi] - csum[:, POOL_PAD - w:POOL_PAD - w + seqlen, gi]
        count = jnp.minimum(t + 1, w).astype(f32)[None, :, None]
        outs.append(win_sum / count - vg[:, :, gi])
    pooled = jnp.stack(outs, axis=2)
    y = jnp.einsum('blgc,gcd->blgd', pooled, w_pool.astype(f32))
    y = y.reshape(bsz, seqlen, POOL_WIDTH) * pool_scale.astype(f32)
    return y.astype(v.dtype)


def setup_inputs(seed: int = 0) -> dict:
    key = jax.random.key(seed)
    ks = iter(jax.random.split(key, 48))
    f32 = jnp.float32

    def nrm(shape, scale):
        return jax.random.normal(next(ks), shape, f32) * scale

    def gain(shape):
        return 1.0 + nrm(shape, 0.02)

    x = nrm((BATCH, SEQ, D_MODEL), 1.0)
    p = nrm((DEPTH, BATCH, SEQ, PLE_DIM), 1.0)
    ffn1_norm = gain((DEPTH, D_MODEL))
    ffn1_w1 = nrm((DEPTH, D_MODEL, D_FF), D_MODEL ** -0.5)
    ffn1_w3 = nrm((DEPTH, D_MODEL, D_FF), D_MODEL ** -0.5)
    ffn1_w2 = nrm((DEPTH, D_FF, D_MODEL), D_FF ** -0.5)
    mix_norm = gain((DEPTH, D_MODEL))
    w_in = nrm((DEPTH, D_MODEL, MIX_WIDTH), D_MODEL ** -0.5)
    a_re = -0.5 + nrm((DEPTH, SSM_GROUPS, SSM_STATE), 0.01)
    a_im = math.pi * jnp.arange(SSM_STATE, dtype=f32)[None, None, :] + nrm((DEPTH, SSM_GROUPS, SSM_STATE), 0.01)
    log_dt = jax.random.uniform(next(ks), (DEPTH, SSM_GROUPS), f32, math.log(DT_MIN), math.log(DT_MAX))
    b_re = nrm((DEPTH, SSM_GROUPS, SSM_STATE, SSM_GROUP), (2 * SSM_GROUP) ** -0.5)
    b_im = nrm((DEPTH, SSM_GROUPS, SSM_STATE, SSM_GROUP), (2 * SSM_GROUP) ** -0.5)
    c_re = nrm((DEPTH, SSM_GROUPS, SSM_GROUP, SSM_STATE), (2 * SSM_STATE) ** -0.5)
    c_im = nrm((DEPTH, SSM_GROUPS, SSM_GROUP, SSM_STATE), (2 * SSM_STATE) ** -0.5)
    d_skip = nrm((DEPTH, SSM_WIDTH), 1.0)
    w_glu = nrm((DEPTH, SSM_WIDTH, SSM_WIDTH), SSM_WIDTH ** -0.5)
    b_glu = nrm((DEPTH, SSM_WIDTH), 0.01)
    w_pool = nrm((DEPTH, POOL_GROUPS, POOL_GROUP_WIDTH, POOL_GROUP_WIDTH), POOL_GROUP_WIDTH ** -0.5)
    pool_scale = gain((DEPTH, POOL_WIDTH))
    ssm_out_norm = gain((DEPTH, SSM_WIDTH))
    pool_out_norm = gain((DEPTH, POOL_WIDTH))
    w_out = nrm((DEPTH, MIX_WIDTH, D_MODEL), MIX_WIDTH ** -0.5)
    ffn2_norm = gain((DEPTH, D_MODEL))
    ffn2_w1 = nrm((DEPTH, D_MODEL, D_FF), D_MODEL ** -0.5)
    ffn2_w3 = nrm((DEPTH, D_MODEL, D_FF), D_MODEL ** -0.5)
    ffn2_w2 = nrm((DEPTH, D_FF, D_MODEL), D_FF ** -0.5)
    ple_gate_norm = gain((DEPTH, D_MODEL))
    w_ple_gate = nrm((DEPTH, D_MODEL, D_MODEL), D_MODEL ** -0.5)
    w_ple_proj = nrm((DEPTH, PLE_DIM, D_MODEL), PLE_DIM ** -0.5)
    ple_norm = gain((DEPTH, D_MODEL))
    final_norm = gain((D_MODEL,))
    return {'x': x, 'p': p,
            'ffn1_norm': ffn1_norm, 'ffn1_w1': ffn1_w1, 'ffn1_w3': ffn1_w3, 'ffn1_w2': ffn1_w2,
            'mix_norm': mix_norm, 'w_in': w_in,
            'a_re': a_re, 'a_im': a_im, 'log_dt': log_dt, 'b_re': b_re, 'b_im': b_im,
            'c_re': c_re, 'c_im': c_im, 'd_skip': d_skip, 'w_glu': w_glu, 'b_glu': b_glu,
            'w_pool': w_pool, 'pool_scale': pool_scale,
            'ssm_out_norm': ssm_out_norm, 'pool_out_norm': pool_out_norm, 'w_out': w_out,
            'ffn2_norm': ffn2_norm, 'ffn2_w1': ffn2_w1, 'ffn2_w3': ffn2_w3, 'ffn2_w2': ffn2_w2,
            'ple_gate_norm': ple_gate_norm, 'w_ple_gate': w_ple_gate, 'w_ple_proj': w_ple_proj,
            'ple_norm': ple_norm, 'final_norm': final_norm}


def reference(x, p, ffn1_norm, ffn1_w1, ffn1_w3, ffn1_w2, mix_norm, w_in,
              a_re, a_im, log_dt, b_re, b_im, c_re, c_im, d_skip, w_glu, b_glu,
              w_pool, pool_scale, ssm_out_norm, pool_out_norm, w_out,
              ffn2_norm, ffn2_w1, ffn2_w3, ffn2_w2,
              ple_gate_norm, w_ple_gate, w_ple_proj, ple_norm, final_norm):
    h = x
    for i in range(DEPTH):
        h = h + FFN_RES * swiglu(rmsnorm(h, ffn1_norm[i]), ffn1_w1[i], ffn1_w3[i], ffn1_w2[i])
        z = rmsnorm(h, mix_norm[i]) @ w_in[i]
        y_ssm = s5_mixer(z[..., :SSM_WIDTH], a_re[i], a_im[i], log_dt[i], b_re[i], b_im[i],
                         c_re[i], c_im[i], d_skip[i], w_glu[i], b_glu[i])
        y_pool = pool_mixer(z[..., SSM_WIDTH:], w_pool[i], pool_scale[i])
        y = jnp.concatenate([rmsnorm(y_ssm, ssm_out_norm[i]),
                             rmsnorm(y_pool, pool_out_norm[i])], axis=-1)
        h = h + y @ w_out[i]
        h = h + FFN_RES * swiglu(rmsnorm(h, ffn2_norm[i]), ffn2_w1[i], ffn2_w3[i], ffn2_w2[i])
        gate = jax.nn.sigmoid(rmsnorm(h, ple_gate_norm[i]) @ w_ple_gate[i])
        emb = rmsnorm(p[i].astype(h.dtype) @ w_ple_proj[i], ple_norm[i])
        h = h + gate * emb
    return rmsnorm(h, final_norm)
```

```python
import math
from contextlib import ExitStack

import numpy as np
import ml_dtypes
import concourse.bass as bass
import concourse.mybir as mybir
from concourse.bass_utils import run_bass_kernel_spmd

F32 = mybir.dt.float32
BF16 = mybir.dt.bfloat16
I32 = mybir.dt.int32
AF = mybir.ActivationFunctionType
ALU = mybir.AluOpType

D = 1024
SEQ = 4096
W = 512
NWIN = SEQ // W
DFF = 2816
NM = DFF // 128
EPS = 1e-6
NSLOT = 4
WF1, WMIX, WF2, WG, WFIN, WY, WT, WO = 12, 8, 35, 30, 25, 30, 20, 25
SLOT = 4096


class Res:
    __slots__ = ("name", "last_write", "readers", "co")

    def __init__(self, name=""):
        self.name = name
        self.last_write = None
        self.readers = []
        self.co = []


class Op:
    __slots__ = ("eng", "fn", "deps", "tick", "has_dep", "is_dma", "sem", "semval", "tag")


class Prog:
    ENGS = ("pe", "act", "dve", "pool", "sp")

    def __init__(self, nc, n_dma_sems=10):
        self.nc = nc
        self.ops = {e: [] for e in self.ENGS}
        self.n_dma_sems = n_dma_sems
        self.dma_rr = {e: 0 for e in self.ENGS}
        self.dma_cnt = {}
        self.dma_last = {}
        self.tag = ""

    def add(self, eng, fn, reads=(), writes=(), dma=False, join=False):
        op = Op()
        op.eng = eng
        op.fn = fn
        op.is_dma = dma
        op.has_dep = False
        op.tag = self.tag
        deps = []
        for r in reads:
            if r.last_write is not None:
                deps.append(r.last_write)
            deps.extend(r.co)
        for w in writes:
            if not join:
                if w.last_write is not None:
                    deps.append(w.last_write)
                deps.extend(w.co)
            deps.extend(w.readers)
        seen = set()
        dl = []
        for d in deps:
            if id(d) in seen:
                continue
            seen.add(id(d))
            if d.eng == "pe" and eng == "pe" and not d.is_dma and not dma:
                continue
            dl.append(d)
        if dma:
            key = (eng, self.dma_rr[eng] % self.n_dma_sems)
            self.dma_rr[eng] += 1
            prev = self.dma_last.get(key)
            if prev is not None and id(prev) not in seen:
                dl.append(prev)
            self.dma_cnt[key] = self.dma_cnt.get(key, 0) + 1
            op.sem = key
            op.semval = 16 * self.dma_cnt[key]
            self.dma_last[key] = op
        for d in dl:
            d.has_dep = True
        op.deps = dl
        for r in reads:
            r.readers.append(op)
        for w in writes:
            if join:
                w.co.append(op)
            else:
                w.last_write = op
                w.co = []
            w.readers = []
        self.ops[eng].append(op)
        return op

    def emit(self, final_waits=()):
        nc = self.nc
        with ExitStack() as st:
            esem = {e: st.enter_context(nc.semaphore("s_" + e)) for e in self.ENGS}
            dsem = {}
            for key in self.dma_cnt:
                dsem[key] = st.enter_context(nc.semaphore("d_%s_%d" % key))
            for e in self.ENGS:
                t = 0
                for op in self.ops[e]:
                    if not op.is_dma and op.has_dep:
                        t += 1
                        op.tick = t
                    else:
                        op.tick = None
            block = st.enter_context(nc.Block())
            prog = self

            def run_engine(ename, eobj):
                waited = {}
                for op in prog.ops[ename]:
                    need = {}
                    for d in op.deps:
                        if d.is_dma:
                            s, v = dsem[d.sem], d.semval
                        else:
                            s, v = esem[d.eng], d.tick
                        k = id(s)
                        if k not in need or need[k][1] < v:
                            need[k] = (s, v)
                    for k, (s, v) in need.items():
                        if waited.get(k, 0) >= v:
                            continue
                        waited[k] = v
                        eobj.wait_ge(s, v)
                    ins = op.fn(eobj)
                    if op.is_dma:
                        ins.then_inc(dsem[op.sem], 16)
                    elif op.has_dep:
                        ins.then_inc(esem[ename], 1)
                if ename == "sp":
                    for op in final_waits:
                        eobj.wait_ge(dsem[op.sem], op.semval)

            @block.tensor
            def _(e):
                run_engine("pe", e)

            @block.scalar
            def _(e):
                run_engine("act", e)

            @block.vector
            def _(e):
                run_engine("dve", e)

            @block.gpsimd
            def _(e):
                run_engine("pool", e)

            @block.sync
            def _(e):
                run_engine("sp", e)


def _kmc(ap, k):
    return ap.rearrange("(k p) (m c) -> p m k c", p=128, c=128)


def build_program(n_win=NWIN, dbg=99):
    nc = bass.Bass("TRN2", target_bir_lowering=False)

    def din(name, shape, dt=F32):
        return nc.dram_tensor(name, list(shape), dt, kind="ExternalInput").ap()

    x = din("x", [SEQ, D])
    pin = din("p", [SEQ, 256])
    wd = {}
    for f in (1, 2):
        wd["ffn%d_w1" % f] = din("ffn%d_w1" % f, [D, DFF])
        wd["ffn%d_w3" % f] = din("ffn%d_w3" % f, [D, DFF])
        wd["ffn%d_w2" % f] = din("ffn%d_w2" % f, [DFF, D])
    wd["w_in"] = din("w_in", [D, D])
    wd["w_out"] = din("w_out", [D, D])
    wd["w_glu"] = din("w_glu", [512, 512])
    wd["w_pool"] = din("w_pool", [4, 128, 128])
    wd["w_ple_gate"] = din("w_ple_gate", [D, D])
    wd["w_ple_proj"] = din("w_ple_proj", [256, D])
    cols_d = din("cols", [128, 64])
    dcol_d = din("dcol", [128, 32])
    s5a_d = din("s5a", [128, 3, 16])
    s5b_d = din("s5b", [128, 2, 16, 16])
    s5c_d = din("s5c", [128, 2, 16, 16])
    ident_d = din("ident", [128, 128])
    cmask_d = din("cmask", [128, 128])
    pcorr_d = din("pcorr", [128, 4, 16])
    fnrow_d = din("fnrow", [128, D])
    out = nc.dram_tensor("out", [SEQ, D], F32, kind="ExternalOutput").ap()

    blocks = {}
    order = []

    def addblk(name, halves):
        blocks[name] = halves
        order.append(name)

    for f in (1, 2):
        w1v = _kmc(wd["ffn%d_w1" % f], 8)
        w3v = _kmc(wd["ffn%d_w3" % f], 8)
        w2v = _kmc(wd["ffn%d_w2" % f], 22)
        for j in range(NM // 2):
            halves = []
            for mm in range(2):
                pcs = []
                for which, wv in enumerate((w1v, w3v)):
                    pcs.append((((mm * 2 + which) * 8) * 128, 8, 128, wv[:, 2 * j + mm, :, :]))
                halves.append(pcs)
            addblk("f%d_up%d" % (f, j), halves)
        for dt_ in range(8):
            addblk("f%d_dn%d" % (f, dt_), [
                [(0, 16, 128, w2v[:, dt_, 0:16, :])],
                [(2048, 6, 128, w2v[:, dt_, 16:22, :])],
            ])
    winv = _kmc(wd["w_in"], 8)
    addblk("win_pool", [
        [((mt * 8) * 128, 8, 128, winv[:, 4 + mt, :, :]) for mt in (0, 1)],
        [((mt * 8) * 128, 8, 128, winv[:, 4 + mt, :, :]) for mt in (2, 3)],
    ])
    winr = wd["w_in"].rearrange("(k p) n -> p k n", p=128)
    addblk("win_ssm", [
        [(0, 4, 512, winr[:, 0:4, 0:512])],
        [(2048, 4, 512, winr[:, 4:8, 0:512])],
    ])
    for nm, key in (("wout", "w_out"), ("gate", "w_ple_gate")):
        wv = _kmc(wd[key], 8)
        for ob in range(2):
            addblk("%s%d" % (nm, ob), [
                [((mt * 8) * 128, 8, 128, wv[:, 4 * ob + mt, :, :]) for mt in (0, 1)],
                [((mt * 8) * 128, 8, 128, wv[:, 4 * ob + mt, :, :]) for mt in (2, 3)],
            ])
    gluv = _kmc(wd["w_glu"], 4)
    addblk("glupool", [
        [((mt * 4) * 128, 4, 128, gluv[:, mt, :, :]) for mt in range(4)],
        [(2048, 4, 128, wd["w_pool"].rearrange("g p d -> p g d"))],
    ])
    prv = _kmc(wd["w_ple_proj"], 2)
    addblk("proj", [
        [((mt * 2) * 128, 2, 128, prv[:, mt, :, :]) for mt in range(8)],
    ])
    blk_idx = {n: i for i, n in enumerate(order)}
    NB = len(order)
    scratch = nc.dram_tensor("wscratch", [NB, 128, SLOT], BF16, kind="Internal").ap()

    with ExitStack() as st:
        def sb(name, shape, dt):
            return st.enter_context(nc.sbuf_tensor("sb_" + name, list(shape), dt))

        hT = sb("hT", [128, 8, W], F32)
        hn = sb("hn", [128, 8, W], BF16)
        yT = sb("yT", [128, 8, W], BF16)
        PT = sb("PT", [128, 32, 128], BF16)
        RTt = sb("RT", [128, 32, 128], BF16)
        Qs = sb("Qs", [128, 16, 2, 128], BF16)
        wslots = sb("wslots", [128, NSLOT, SLOT], BF16)
        Etab = sb("Etab", [128, 2, 64, 16], F32)
        rho = sb("rho", [128, 16], F32)
        dummy = sb("dummy", [128, 2], F32)
        xtok = sb("xtok", [128, 2, D], F32)
        otok = sb("otok", [128, 2, D], F32)
        ptok = sb("ptok", [128, 4, 256], F32)
        pT = sb("pT", [128, 2, W], BF16)
        silu_t = sb("silu_t", [128, 2, W], F32)
        rstd = sb("rstd", [128, W], F32)
        gtmp = sb("gtmp", [128, W], F32)
        HB = sb("HB", [128, 28, W], BF16)
        BB = sb("BB", [128, 8, W], F32)
        CC = sb("CC", [128, 6240], F32)
        cols = sb("cols", [128, 64], F32)
        dcol = sb("dcol", [128, 32], F32)
        rstd_e = sb("rstd_e", [128, W], F32)
        pscw = sb("pscw", [128, 4], F32)
        fnrow = sb("fnrow", [128, D], F32)
        facc = sb("facc", [128, 16], F32)
        tC = sb("tC", [128, 512], F32)
        ident = sb("ident", [128, 128], F32)
        identb = sb("identb", [128, 128], BF16)
        onesb = sb("onesb", [128, 128], BF16)
        warmt = sb("warmt", [128, W], BF16)
        cmask = sb("cmask", [128, 128], F32)
        pcorr = sb("pcorr", [128, 4, 16], F32)
        s5a = sb("s5a", [128, 3, 16], F32)
        HBf = HB[:].rearrange("p a b -> p (a b)").bitcast(F32)
        s5b = HBf[:, 0:512].rearrange("p (a g c) -> p a g c", a=2, g=16)
        s5c = HBf[:, 512:1024].rearrange("p (a g c) -> p a g c", a=2, g=16)
        sm = HBf[:, 1024:1664].rearrange("p (a c) -> p a c", a=40)
        smi = HBf[:, 1664:1680].bitcast(I32)
        pw = HBf[:, 1696:1984].rearrange("p (k r g) -> p k r g", k=9, r=2)
        bbar = HBf[:, 2048:2560].rearrange("p (a g c) -> p a g c", a=2, g=16)
        Wc = sb("Wc", [128, 32], F32)
        psum = st.enter_context(nc.psum_tensor("psum", [128, 8, W], F32))

        P = Prog(nc)

        R_hT = [Res("hT%d" % k) for k in range(8)]
        R_hn = [Res("hn%d" % k) for k in range(8)]
        R_yT = [Res("yT%d" % k) for k in range(8)]
        R_H = [Res("H%d" % k) for k in range(28)]
        R_B = [Res("B%d" % k) for k in range(8)]
        R_ps = [Res("ps%d" % k) for k in range(8)]
        R_slot = [Res("slot%d" % k) for k in range(NSLOT)]
        R_stage = []
        R_xtok = [Res("xtok0"), Res("xtok1")]
        R_otok = [Res("otok0"), Res("otok1")]
        R_scr = [Res("scr%d" % k) for k in range(NB)]
        R_ops = Res("s5ops")
        R_const = Res("const")
        R_sm = Res("sm")
        R_ptok = Res("ptok")
        R_pT = Res("pT")
        R_silu = [Res("silu0"), Res("silu1")]
        R_rstd = Res("rstd")
        R_gtmp = Res("gtmp")
        R_vpool = Res("vpool")
        R_ptmp = Res("ptmp")
        R_yf = Res("yf")
        R_state = Res("state")
        R_st = Res("st")
        R_out = Res("out")

        hid = HB
        sq = HB
        Z2f = HB[0:64, 8:16, :].rearrange("p a b -> p (a b)")
        Z2g = Z2f.rearrange("p (g s c) -> p g s c", g=32, s=8)
        Y2bf = HB[0:64, 0:8, :]
        R_Y2 = R_H[0:8]
        U = HB[:, 16:20, :].rearrange("p a b -> p (a b)").rearrange("p (g c) -> p g c", g=32)
        Sabf = HB[:, 20:24, :].rearrange("p a b -> p (a b)").rearrange("p (r c) -> p r c", r=32)
        ysb = HB[:, 24:28, :]
        R_Z2 = R_H[8:16]
        R_U = R_H[16:20]
        R_Sabf = R_H[20:24]
        R_ysb = R_H[24:28]
        BBf = BB[:].rearrange("p a b -> p (a b)")
        VV = BBf[:, 0:2048].rearrange("p (c r) -> p c r", r=32)
        ebuf = BB
        vpool = CC[:, 0:2112].rearrange("p (g l) -> p g l", g=4)
        tA = CC[:, 2112:2640]
        tB = CC[:, 2640:3168]
        pooled = CC[:, 3168:4192].bitcast(BF16).rearrange("p (g l) -> p g l", g=4)
        gel = CC[0:64, 4192:6240].rearrange("p (a l) -> p a l", a=4)
        yf = CC[:, 4192:6240].rearrange("p (g l) -> p g l", g=4)

        sqA = yT
        R_sqA = R_yT
        sqB = otok[:].rearrange("p a b -> p (a b)").bitcast(BF16).rearrange("p (k t) -> p k t", k=8)
        R_sqB = [R_otok[0]] * 4 + [R_otok[1]] * 4
        e_bf = xtok[:].rearrange("p a b -> p (a b)").bitcast(BF16).rearrange("p (k t) -> p k t", k=8)
        R_ebf = [R_xtok[0]] * 4 + [R_xtok[1]] * 4
        R_rstde = Res("rstd_e")
        R_gel = [Res("gel0"), Res("gel1")]
        R_ptmp_all = [R_ptmp]
        R_pooled = Res("pooled")
        R_yfm = [Res("yfm%d" % k) for k in range(4)]
        R_yf_all = [R_yf] + R_gel + R_yfm

        bank_rr = [0]

        def bank():
            b = bank_rr[0] % 7
            bank_rr[0] += 1
            return b

        flip = [0]

        def evac_eng():
            flip[0] += 1
            return "act" if flip[0] % 2 else "dve"

        def warm(n):
            for _ in range(n):
                P.add("pe", lambda e: e.matmul(psum[:, 7, :], lhsT=onesb[:], rhs=warmt[:], start=True, stop=True))

        def copy_op(eng, out_ap, in_ap, reads, writes):
            if eng == "act":
                P.add("act", lambda e: e.copy(out=out_ap, in_=in_ap), reads=reads, writes=writes)
            else:
                P.add(eng, lambda e: e.tensor_copy(out=out_ap, in_=in_ap), reads=reads, writes=writes)

        def tt(eng, out_ap, a, b, op, reads, writes):
            P.add(eng, lambda e: e.tensor_tensor(out=out_ap, in0=a, in1=b, op=op), reads=reads, writes=writes)

        def ts(eng, out_ap, a, s1, s2, op0, op1, reads, writes):
            if s2 is None:
                P.add(eng, lambda e: e.tensor_scalar(out=out_ap, in0=a, scalar1=s1, scalar2=None, op0=op0), reads=reads, writes=writes)
            else:
                P.add(eng, lambda e: e.tensor_scalar(out=out_ap, in0=a, scalar1=s1, scalar2=s2, op0=op0, op1=op1), reads=reads, writes=writes)

        def stt(out_ap, a, scalar, b, op0, op1, reads, writes):
            P.add("dve", lambda e: e.scalar_tensor_tensor(out=out_ap, in0=a, scalar=scalar, in1=b, op0=op0, op1=op1), reads=reads, writes=writes)

        def act(out_ap, in_ap, func, reads, writes, scale=1.0, bias=0.0):
            P.add("act", lambda e: e.activation(out=out_ap, in_=in_ap, func=func, bias=bias, scale=scale), reads=reads, writes=writes)

        def mm(out_ap, lhsT, rhs, start, stop, reads, writes):
            P.add("pe", lambda e: e.matmul(out_ap, lhsT=lhsT, rhs=rhs, start=start, stop=stop), reads=reads, writes=writes)

        def tr(out_ap, in_ap, idn, reads, writes):
            P.add("pe", lambda e: e.transpose(out_ap, in_ap, idn), reads=reads, writes=writes)

        for dst, src in ((cols[:], cols_d), (dcol[:], dcol_d), (ident[:], ident_d), (cmask[:], cmask_d),
                         (pcorr[:], pcorr_d), (fnrow[:], fnrow_d), (s5a[:], s5a_d), (s5b, s5b_d), (s5c, s5c_d)):
            P.add("sp", lambda e, dst=dst, src=src: e.dma_start(out=dst, in_=src), writes=[R_const], dma=True, join=True)
        P.add("dve", lambda e: e.tensor_copy(out=identb[:], in_=ident[:]), reads=[R_const], writes=[R_const])
        P.add("dve", lambda e: e.memset(onesb[:], 1.0), writes=[R_const])
        P.add("dve", lambda e: e.memset(warmt[:], 1.0), writes=[R_const])
        for gi_, wdw_ in enumerate((2, 4, 8, 16)):
            P.add("dve", lambda e, gi_=gi_, wdw_=wdw_: e.tensor_scalar(out=pscw[:, gi_:gi_ + 1], in0=cols[:, 56 + gi_:57 + gi_],
                                                                      scalar1=1.0 / wdw_, scalar2=None, op0=ALU.mult),
                  reads=[R_const], writes=[R_const])
        P.add("pool", lambda e: e.memset(Wc[:], 0.0), writes=[R_state])

        C_FFN1, C_MIX, C_FFN2, C_GATE, C_PLE, C_FIN = 0, 8, 16, 24, 32, 40
        C_SSMN, C_POOLN, C_PSCALE, C_BGLU = 48, 52, 56, 60

        P.tag = "pro"
        def S(i):
            return sm[:, i, :]

        rc = [R_const, R_sm]

        def v_tt(o, a, b, op):
            tt("dve", o, a, b, op, rc, [R_sm])

        def v_ts(o, a, s1, s2, op0, op1=None):
            ts("dve", o, a, s1, s2, op0, op1, rc, [R_sm])

        are, aim, ldt = s5a[:, 0, :], s5a[:, 1, :], s5a[:, 2, :]
        act(S(0), ldt, AF.Exp, rc, [R_sm])
        v_tt(S(1), are, S(0), ALU.mult)
        v_tt(S(2), aim, S(0), ALU.mult)
        act(S(3), S(1), AF.Exp, rc, [R_sm])

        def sincos(dst, src_ang, extra):
            v_ts(S(30), src_ang, 1.0 / (2 * math.pi), 8.5 + extra, ALU.mult, ALU.add)
            P.add("dve", lambda e: e.tensor_copy(out=smi, in_=S(30)), reads=rc, writes=[R_sm])
            P.add("dve", lambda e: e.tensor_copy(out=S(31), in_=smi), reads=rc, writes=[R_sm])
            v_tt(S(30), S(30), S(31), ALU.subtract)
            v_ts(S(31), S(30), 0.0, None, ALU.is_lt)
            v_tt(S(30), S(30), S(31), ALU.add)
            act(dst, S(30), AF.Sin, rc, [R_sm], scale=2 * math.pi, bias=-math.pi)

        sincos(S(4), S(2), 0.0)
        sincos(S(5), S(2), 0.25)
        a_r, a_i = pw[:, 1, 0, :], pw[:, 1, 1, :]
        v_tt(a_r, S(3), S(5), ALU.mult)
        v_tt(a_i, S(3), S(4), ALU.mult)
        P.add("dve", lambda e: e.memset(pw[:, 0, 0, :], 1.0), reads=rc, writes=[R_sm])
        P.add("dve", lambda e: e.memset(pw[:, 0, 1, :], 0.0), reads=rc, writes=[R_sm])
        for k in range(2, 9):
            pr, pi_, qr, qi = pw[:, k - 1, 0, :], pw[:, k - 1, 1, :], pw[:, k, 0, :], pw[:, k, 1, :]
            v_tt(S(6), pr, a_r, ALU.mult)
            v_tt(S(7), pi_, a_i, ALU.mult)
            v_tt(qr, S(6), S(7), ALU.subtract)
            v_tt(S(6), pr, a_i, ALU.mult)
            v_tt(S(7), pi_, a_r, ALU.mult)
            v_tt(qi, S(6), S(7), ALU.add)
        v_tt(S(6), are, are, ALU.mult)
        v_tt(S(7), aim, aim, ALU.mult)
        v_tt(S(6), S(6), S(7), ALU.add)
        P.add("dve", lambda e: e.reciprocal(out=S(8), in_=S(6)), reads=rc, writes=[R_sm])
        v_ts(S(9), a_r, -1.0, None, ALU.add)
        v_tt(S(6), S(9), are, ALU.mult)
        v_tt(S(7), a_i, aim, ALU.mult)
        v_tt(S(6), S(6), S(7), ALU.add)
        v_tt(S(10), S(6), S(8), ALU.mult)
        v_tt(S(6), a_i, are, ALU.mult)
        v_tt(S(7), S(9), aim, ALU.mult)
        v_tt(S(6), S(6), S(7), ALU.subtract)
        v_tt(S(11), S(6), S(8), ALU.mult)
        p8r, p8i = pw[:, 8, 0, :], pw[:, 8, 1, :]
        v_tt(S(6), p8r, p8r, ALU.mult)
        v_tt(S(7), p8i, p8i, ALU.mult)
        v_tt(S(6), S(6), S(7), ALU.add)
        P.add("dve", lambda e: e.reciprocal(out=S(7), in_=S(6)), reads=rc, writes=[R_sm])
        v_tt(S(12), p8r, S(7), ALU.mult)
        v_tt(S(13), p8i, S(7), ALU.mult)
        v_ts(S(13), S(13), -1.0, None, ALU.mult)
        act(rho[:], S(1), AF.Exp, rc, [R_sm], scale=8.0)
        act(S(32), S(1), AF.Exp, rc, [R_sm], scale=-8.0)
        v_tt(Etab[:, 0, 0, :], p8r, S(32), ALU.mult)
        v_tt(S(33), p8i, S(32), ALU.mult)
        v_ts(Etab[:, 1, 0, :], S(33), -1.0, None, ALU.mult)
        TE = HBf[:, 6656:7168].rearrange("p (c g) -> p c g", g=16)
        for L in (1, 2, 4, 8, 16, 32):
            er0, ei0 = Etab[:, 0, 0:L, :], Etab[:, 1, 0:L, :]
            br = Etab[:, 0, L - 1, :].unsqueeze(1).to_broadcast([128, L, 16])
            bi = Etab[:, 1, L - 1, :].unsqueeze(1).to_broadcast([128, L, 16])
            dr, di = Etab[:, 0, L:2 * L, :], Etab[:, 1, L:2 * L, :]
            tmpE = TE[:, 0:L, :]
            v_tt(dr, er0, br, ALU.mult)
            v_tt(tmpE, ei0, bi, ALU.mult)
            v_tt(dr, dr, tmpE, ALU.subtract)
            v_tt(di, er0, bi, ALU.mult)
            v_tt(tmpE, ei0, br, ALU.mult)
            v_tt(di, di, tmpE, ALU.add)

        def bc16(ap2):
            return ap2.unsqueeze(2).to_broadcast([128, 16, 16])

        T0 = sm[:, 14:30, :]
        bre, bim = s5b[:, 0, :, :], s5b[:, 1, :, :]
        bbr, bbi = bbar[:, 0, :, :], bbar[:, 1, :, :]
        v_tt(bbr, bre, bc16(S(10)), ALU.mult)
        v_tt(T0, bim, bc16(S(11)), ALU.mult)
        v_tt(bbr, bbr, T0, ALU.subtract)
        v_tt(bbi, bim, bc16(S(10)), ALU.mult)
        v_tt(T0, bre, bc16(S(11)), ALU.mult)
        v_tt(bbi, bbi, T0, ALU.add)

        def big(ap2):
            return ap2.rearrange("p (g x) -> p g x", g=16)

        Pr = big(BBf[:, 0:2048])
        Pi = big(BBf[:, 2048:4096])
        P2r = big(CC[:, 0:2048])
        P2i = big(CC[:, 2048:4096])
        Qr = big(HBf[:, 2560:4608])
        Qni = big(HBf[:, 4608:6656])
        Tb = big(CC[:, 4096:6144])
        R_pro = R_B + [R_vpool, R_ptmp, R_gel[0], R_gel[1], R_yf] + [R_sm, R_const]

        def b_tt(o, a, b, op):
            tt("dve", o, a, b, op, R_pro, R_pro)

        T1 = sm[:, 14:30, :]
        for s in range(8):
            k = 7 - s
            o_r = Pr[:, :, s * 16:(s + 1) * 16]
            o_i = Pi[:, :, s * 16:(s + 1) * 16]
            wr, wi = bc16(pw[:, k, 0, :]), bc16(pw[:, k, 1, :])
            b_tt(o_r, bbr, wr, ALU.mult)
            b_tt(T1, bbi, wi, ALU.mult)
            b_tt(o_r, o_r, T1, ALU.subtract)
            b_tt(o_i, bbi, wr, ALU.mult)
            b_tt(T1, bbr, wi, ALU.mult)
            b_tt(o_i, o_i, T1, ALU.add)
        for t in range(8):
            k = t + 1
            o_r = Qr[:, :, t * 16:(t + 1) * 16]
            o_i = Qni[:, :, t * 16:(t + 1) * 16]
            wr, wi = bc16(pw[:, k, 0, :]), bc16(pw[:, k, 1, :])
            cre, cim = s5c[:, 0, :, :], s5c[:, 1, :, :]
            b_tt(o_r, cre, wr, ALU.mult)
            b_tt(T1, cim, wi, ALU.mult)
            b_tt(o_r, o_r, T1, ALU.subtract)
            b_tt(o_i, cre, wi, ALU.mult)
            b_tt(T1, cim, wr, ALU.mult)
            b_tt(o_i, o_i, T1, ALU.add)
        ts("dve", Qni, Qni, -1.0, None, ALU.mult, None, R_pro, R_pro)

        def bc128(ap2):
            return ap2.unsqueeze(2).to_broadcast([128, 16, 128])

        b_tt(P2r, Pr, bc128(S(12)), ALU.mult)
        b_tt(Tb, Pi, bc128(S(13)), ALU.mult)
        b_tt(P2r, P2r, Tb, ALU.subtract)
        b_tt(P2i, Pi, bc128(S(12)), ALU.mult)
        b_tt(Tb, Pr, bc128(S(13)), ALU.mult)
        b_tt(P2i, P2i, Tb, ALU.add)
        P.add("dve", lambda e: e.tensor_copy(out=Qs[:, :, 0, :], in_=Qr), reads=R_pro, writes=[R_ops])
        P.add("dve", lambda e: e.tensor_copy(out=Qs[:, :, 1, :], in_=Qni), reads=R_pro, writes=[R_ops])
        for g in range(32):
            half, gp = g // 16, g % 16
            b = bank()
            hs = slice(half * 64, half * 64 + 64)
            for ri, Px in enumerate((Pr, Pi)):
                tr(psum[:, b, ri * 64:(ri + 1) * 64], Px[hs, gp, :], ident[hs, hs], R_pro, [R_ps[b]])
            pe_ = evac_eng()
            if pe_ == "act":
                P.add("act", lambda e, g=g, b=b: e.copy(out=PT[:, g, :], in_=psum[:, b, 0:128]), reads=[R_ps[b]], writes=[R_ops], join=True)
            else:
                P.add("dve", lambda e, g=g, b=b: e.tensor_copy(out=PT[:, g, :], in_=psum[:, b, 0:128]), reads=[R_ps[b]], writes=[R_ops], join=True)
            b2 = bank()
            mm(psum[:, b2, 0:128], P2r[hs, gp, :], Qr[hs, gp, :], True, False, R_pro, [R_ps[b2]])
            mm(psum[:, b2, 0:128], P2i[hs, gp, :], Qni[hs, gp, :], False, True, R_pro, [R_ps[b2]])
            tt("dve", gtmp[:, 0:128], psum[:, b2, 0:128], cmask[:], ALU.mult, [R_ps[b2], R_const], [R_gtmp])
            P.add("dve", lambda e, g=g: e.scalar_tensor_tensor(out=RTt[:, g, :], in0=ident[:], scalar=dcol[:, g:g + 1], in1=gtmp[:, 0:128],
                                                              op0=ALU.mult, op1=ALU.add),
                  reads=[R_gtmp, R_const], writes=[R_ops], join=True)

        P.add("pool", lambda e: e.memset(vpool[:, :, 0:16], 0.0), reads=R_pro, writes=[R_vpool])
        P.add("dve", lambda e: e.memset(dummy[:], 0.0), reads=[R_ops], writes=[R_sm, R_const, R_ops] + R_H)

        slot_rr = [0]
        cvt_flip = [0]

        def wload(name, first):
            bi = blk_idx[name]
            si = slot_rr[0] % NSLOT
            slot_rr[0] += 1
            sl = wslots[:, si, :]
            if first:
                n = max(pc[0] + pc[1] * pc[2] for pcs in blocks[name] for pc in pcs)
                first_piece = True
                for pcs in blocks[name]:
                    for (off, k, c, src) in pcs:
                        dst = sl[:, off:off + k * c].rearrange("p (k c) -> p k c", k=k)
                        P.add("pool", lambda e, dst=dst, src=src: e.dma_start(out=dst, in_=src),
                              writes=[R_slot[si]], dma=True, join=not first_piece)
                        first_piece = False
                P.add("sp", lambda e, sl=sl, bi=bi, n=n: e.dma_start(out=scratch[bi, :, 0:n], in_=sl[:, 0:n]),
                      reads=[R_slot[si]], writes=[R_scr[bi]], dma=True)
            else:
                n = max(pc[0] + pc[1] * pc[2] for pcs in blocks[name] for pc in pcs)
                qeng = "sp" if (slot_rr[0] % 2) else "pool"
                P.add(qeng, lambda e, sl=sl, bi=bi, n=n: e.dma_start(out=sl[:, 0:n], in_=scratch[bi, :, 0:n]),
                      reads=[R_scr[bi]], writes=[R_slot[si]], dma=True)
            return sl, R_slot[si]

        def stats_rstd(src_chunks, nchunk, reads_src, inv_n, presq=None, outt=None):
            b = bank()
            sqv, Rsq = (sq, R_H) if presq is None else presq
            for k in range(nchunk):
                if presq is None:
                    act(sqv[:, k, :], src_chunks(k), AF.Square, [reads_src[k]], [Rsq[k]])
                mm(psum[:, b, :], onesb[:], sqv[:, k, :], k == 0, k == nchunk - 1, [Rsq[k], R_const], [R_ps[b]])
            ro_, Rro_ = (rstd[:], [R_rstd]) if outt is None else outt
            act(ro_, psum[:, b, :], AF.Sqrt, [R_ps[b]], Rro_, scale=inv_n, bias=EPS)
            P.add("dve", lambda e: e.reciprocal(out=ro_, in_=ro_), reads=Rro_, writes=Rro_)

        def norm_full(gc, presq=None):
            stats_rstd(lambda k: hT[:, k, :], 8, R_hT, 1.0 / D, presq)
            for k in range(8):
                stt(hn[:, k, :], hT[:, k, :], cols[:, gc + k:gc + k + 1], rstd[:], ALU.mult, ALU.mult,
                    [R_hT[k], R_rstd, R_const], [R_hn[k]])

        def norm_gain_only(gc, presq):
            for k in range(8):
                ts("dve", hn[:, k, :], hT[:, k, :], cols[:, gc + k:gc + k + 1], None, ALU.mult, None, [R_hT[k], R_const], [R_hn[k]])
            stats_rstd(lambda k: hT[:, k, :], 8, R_hT, 1.0 / D, presq)

        def ffn(f, first, sqnext=None):
            for j in range(NM // 2):
                sl, rs = wload("f%d_up%d" % (f, j), first)
                bks = [(bank(), bank()) for _ in range(2)]
                if j == 0:
                    for k in range(8):
                        for mm_ in range(2):
                            for which in range(2):
                                o = ((mm_ * 2 + which) * 8 + k) * 128
                                b = bks[mm_][which]
                                mm(psum[:, b, :], sl[:, o:o + 128], hn[:, k, :], k == 0, k == 7, [rs, R_hn[k]], [R_ps[b]])
                else:
                    for mm_ in range(2):
                        for which in range(2):
                            b = bks[mm_][which]
                            for k in range(8):
                                o = ((mm_ * 2 + which) * 8 + k) * 128
                                mm(psum[:, b, :], sl[:, o:o + 128], hn[:, k, :], k == 0, k == 7, [rs, R_hn[k]], [R_ps[b]])
                for mm_ in range(2):
                    m = 2 * j + mm_
                    bA, bB = bks[mm_]
                    st_ = silu_t[:, m % 2, :]
                    Rs_ = [R_silu[m % 2]]
                    tt("dve", st_, psum[:, bA, :], rstd[:], ALU.mult, [R_ps[bA], R_rstd], Rs_)
                    act(st_, st_, AF.Silu, Rs_, Rs_)
                    tt("dve", st_, st_, rstd[:], ALU.mult, Rs_ + [R_rstd], Rs_)
                    tt("dve", hid[:, m, :], st_, psum[:, bB, :], ALU.mult, Rs_ + [R_ps[bB]], [R_H[m]])
            for dt_ in range(8):
                sl, rs = wload("f%d_dn%d" % (f, dt_), first)
                b = bank()
                for k in range(NM):
                    mm(psum[:, b, :], sl[:, k * 128:(k + 1) * 128], hid[:, k, :], k == 0, k == NM - 1, [rs, R_H[k]], [R_ps[b]])
                stt(hT[:, dt_, :], psum[:, b, :], 0.5, hT[:, dt_, :], ALU.mult, ALU.add, [R_ps[b], R_hT[dt_]], [R_hT[dt_]])
                if sqnext is not None:
                    act(sqnext[0][:, dt_, :], hT[:, dt_, :], AF.Square, [R_hT[dt_]], [sqnext[1][dt_]])

        def load_x_piece(w, bq):
            t0 = w * W + bq * 128
            P.add("sp", lambda e: e.dma_start(out=xtok[:, bq % 2, :], in_=x[t0:t0 + 128, :]), writes=[R_xtok[bq % 2]], dma=True)

        final_ops = []

        for w in range(n_win):
            first = (w == 0)
            t0w = w * W
            P.tag = "X"
            if w == 0:
                load_x_piece(0, 0)
                load_x_piece(0, 1)
            P.add("sp", lambda e, t0w=t0w: e.dma_start(out=ptok[:], in_=pin[t0w:t0w + W, :].rearrange("(b p) d -> p b d", p=128)),
                  writes=[R_ptok], dma=True)
            for bq in range(4):
                for hb in range(2):
                    b = bank()
                    for kk in range(4):
                        k = hb * 4 + kk
                        tr(psum[:, b, kk * 128:(kk + 1) * 128], xtok[:, bq % 2, k * 128:(k + 1) * 128], ident[:],
                           [R_xtok[bq % 2], R_const], [R_ps[b]])
                    copy_op(evac_eng(), hT[:, hb * 4:hb * 4 + 4, bq * 128:(bq + 1) * 128],
                            psum[:, b, :].rearrange("p (a c) -> p a c", a=4), [R_ps[b]], R_hT[hb * 4:hb * 4 + 4])
                    if dbg >= 1:
                        act(sqA[:, hb * 4:hb * 4 + 4, bq * 128:(bq + 1) * 128], hT[:, hb * 4:hb * 4 + 4, bq * 128:(bq + 1) * 128],
                            AF.Square, R_hT[hb * 4:hb * 4 + 4], R_sqA[hb * 4:hb * 4 + 4])
                if bq + 2 < 4:
                    load_x_piece(w, bq + 2)
            if dbg >= 1:
                P.tag = "F1"
                norm_gain_only(C_FFN1, (sqA, R_sqA))
                ffn(1, first, (sqA, R_sqA) if dbg >= 2 else None)
            if dbg >= 2:
                P.tag = "mixin"
                norm_full(C_MIX, (sqA, R_sqA))
                warm(WMIX)
                sl, rs = wload("win_ssm", first)
                sb4 = [bank() for _ in range(4)]
                for k in range(8):
                    for s in range(4):
                        mm(psum[0:64, sb4[s], :], hn[:, k, s:W:8], sl[:, k * 512:(k + 1) * 512], k == 0, k == 7, [rs, R_hn[k]], [R_ps[sb4[s]]])
                for s in range(8):
                    if s < 4:
                        b = sb4[s]
                    else:
                        b = bank()
                        for k in range(8):
                            mm(psum[0:64, b, :], hn[:, k, s:W:8], sl[:, k * 512:(k + 1) * 512], k == 0, k == 7, [rs, R_hn[k]], [R_ps[b]])
                    copy_op("act", Z2g[:, :, s, :], psum[0:64, b, :].rearrange("p (g c) -> p g c", g=32), [R_ps[b]], R_Z2)
                slg, rsg = wload("glupool", first)
                P.tag = "U"
                psb = psum[:].rearrange("p a b -> p (a b)").bitcast(BF16).rearrange("p (a b) -> p a b", a=8)
                for gb in range(2):
                    b = bank()
                    for gg in range(16):
                        g = gb * 16 + gg
                        tr(psb[:, b, gg * 64:(gg + 1) * 64], Z2f[:, g * 128:(g + 1) * 128], identb[0:64, 0:64],
                           R_Z2 + [R_const], [R_ps[b]])
                    copy_op("act", U[:, gb * 16:(gb + 1) * 16, :], psb[:, b, :].rearrange("p (g c) -> p g c", g=16),
                            [R_ps[b]], R_U)
                P.tag = "V"
                vb = [bank() for _ in range(4)]
                for g in range(32):
                    half, gp = g // 16, g % 16
                    for ri in range(2):
                        col = (ri * 16 + gp) * 64
                        b = vb[col // 512]
                        mm(psum[half * 64:(half + 1) * 64, b, col % 512:col % 512 + 64], PT[:, g, ri * 64:(ri + 1) * 64], U[:, g, :],
                           True, True, R_U + [R_ops], [R_ps[b]])
                for q in range(4):
                    b = vb[q]
                    copy_op("act", VV[:, :, q * 8:(q + 1) * 8].rearrange("p c r -> p r c"),
                            psum[:, b, :].rearrange("p (r c) -> p r c", r=8), [R_ps[b]], R_B)
                P.tag = "scan"
                Wd = BBf[:, 2048:4096].rearrange("p (c r) -> p c r", r=32)
                Er, Ei = Etab[:, 0, :, :], Etab[:, 1, :, :]
                tq1 = CC[:, 4192:5216].rearrange("p (c g) -> p c g", g=16)
                tq2 = CC[:, 5216:6240].rearrange("p (c g) -> p c g", g=16)
                RV, RW = R_B[0:4], R_B[4:8]
                ro = [R_ops]
                tt("dve", Wd[:, :, 0:16], Er, VV[:, :, 0:16], ALU.mult, RV + ro, RW)
                tt("dve", tq1, Ei, VV[:, :, 16:32], ALU.mult, RV + ro, R_yf_all)
                tt("dve", Wd[:, :, 16:32], Er, VV[:, :, 16:32], ALU.mult, RV + ro, RW)
                tt("dve", tq2, Ei, VV[:, :, 0:16], ALU.mult, RV + ro + R_yf_all, R_yf_all)
                tt("dve", Wd[:, :, 0:16], Wd[:, :, 0:16], tq1, ALU.subtract, RW + R_yf_all, RW)
                tt("dve", Wd[:, :, 16:32], Wd[:, :, 16:32], tq2, ALU.add, RW + R_yf_all, RW)
                P.add("dve", lambda e: e.tensor_copy(out=Sabf[:, :, 0], in_=Wc[:]), reads=[R_state], writes=R_Sabf)
                for r in range(32):
                    gp = r % 16
                    P.add("dve", lambda e, r=r, gp=gp: e.tensor_tensor_scan(
                        out=VV[:, :, r], data0=rho[:, gp:gp + 1].to_broadcast([128, 64]), data1=Wd[:, :, r],
                        initial=Wc[:, r:r + 1], op0=ALU.mult, op1=ALU.add),
                        reads=RW + [R_state, R_ops], writes=RV, join=(r > 0))
                tt("dve", Wd[:, :, 0:16], Er, VV[:, :, 0:16], ALU.mult, RV + ro, RW)
                tt("dve", tq1, Ei, VV[:, :, 16:32], ALU.mult, RV + ro + R_yf_all, R_yf_all)
                tt("dve", Wd[:, :, 16:32], Er, VV[:, :, 16:32], ALU.mult, RV + ro, RW)
                tt("dve", tq2, Ei, VV[:, :, 0:16], ALU.mult, RV + ro + R_yf_all, R_yf_all)
                tt("dve", Wd[:, :, 0:16], Wd[:, :, 0:16], tq1, ALU.add, RW + R_yf_all, RW)
                tt("dve", Wd[:, :, 16:32], Wd[:, :, 16:32], tq2, ALU.subtract, RW + R_yf_all, RW)
                P.add("dve", lambda e: e.tensor_copy(out=Sabf[:, :, 1:64], in_=Wd[:, 0:63, :].rearrange("p c r -> p r c")),
                      reads=RW, writes=R_Sabf)
                P.add("pool", lambda e: e.tensor_copy(out=Wc[:], in_=Wd[:, 63, :]), reads=RW + R_Sabf, writes=[R_state])
                P.tag = "Y"
                P.tag = "zpool"
                sl, rs = wload("win_pool", first)
                zb = [bank() for _ in range(4)]
                for k in range(8):
                    for mt in range(4):
                        o = (mt * 8 + k) * 128
                        mm(psum[:, zb[mt], :], sl[:, o:o + 128], hn[:, k, :], k == 0, k == 7, [rs, R_hn[k]], [R_ps[zb[mt]]])
                for mt in range(4):
                    copy_op("act", vpool[:, mt, 16:528], psum[:, zb[mt], :], [R_ps[zb[mt]]], [R_vpool])
                P.tag = "poolel"
                for gi, wdw in enumerate((2, 4, 8, 16)):
                    v = vpool[:, gi, :]
                    L = 528
                    rv = [R_vpool, R_ptmp] + R_gel
                    tt("pool", tA[:, 1:L], v[:, 1:L], v[:, 0:L - 1], ALU.add, rv, R_ptmp_all)
                    cur = tA
                    if wdw >= 4:
                        tt("pool", tB[:, 3:L], tA[:, 3:L], tA[:, 1:L - 2], ALU.add, rv, R_ptmp_all)
                        cur = tB
                    if wdw >= 8:
                        tt("pool", tA[:, 7:L], tB[:, 7:L], tB[:, 3:L - 4], ALU.add, rv, R_ptmp_all)
                        cur = tA
                    if wdw >= 16:
                        tt("pool", tB[:, 15:L], tA[:, 15:L], tA[:, 7:L - 8], ALU.add, rv, R_ptmp_all)
                        cur = tB
                    if first:
                        tt("pool", cur[:, 16:32], cur[:, 16:32], pcorr[:, gi, :], ALU.mult, rv + [R_const], R_ptmp_all)
                    tt("pool", tC[:, 0:512], v[:, 16:L], v[:, 16:L], ALU.add, rv, R_ptmp_all)
                    for _ in range(gi):
                        tt("pool", tC[:, 0:512], tC[:, 0:512], tC[:, 0:512], ALU.add, rv, R_ptmp_all)
                    tt("pool", pooled[:, gi, :], cur[:, 16:L], tC[:, 0:512], ALU.subtract, rv, [R_pooled])
                P.add("pool", lambda e: e.tensor_copy(out=vpool[:, :, 0:16], in_=vpool[:, :, 512:528]), reads=[R_vpool, R_ptmp] + R_gel, writes=[R_vpool])
                P.tag = "epath"
                for hb in range(2):
                    b = bank()
                    for q in range(4):
                        idx = hb * 4 + q
                        bq, kc = idx // 2, idx % 2
                        tr(psum[:, b, q * 128:(q + 1) * 128], ptok[:, bq, kc * 128:(kc + 1) * 128], ident[:], [R_ptok, R_const], [R_ps[b]])
                    for q in range(4):
                        idx = hb * 4 + q
                        bq, kc = idx // 2, idx % 2
                        copy_op("act", pT[:, kc, bq * 128:(bq + 1) * 128], psum[:, b, q * 128:(q + 1) * 128], [R_ps[b]], [R_pT])
                sl, rs = wload("proj", first)
                for mt in range(8):
                    b = bank()
                    for k in range(2):
                        o = (mt * 2 + k) * 128
                        mm(psum[:, b, :], sl[:, o:o + 128], pT[:, k, :], k == 0, k == 1, [rs, R_pT], [R_ps[b]])
                    act(e_bf[:, mt, :], psum[:, b, :], AF.Copy, [R_ps[b], R_const], [R_ebf[mt]], scale=cols[:, C_PLE + mt:C_PLE + mt + 1])
                    act(sq[:, mt, :], psum[:, b, :], AF.Square, [R_ps[b]], [R_H[mt]])
                bst = bank()
                for mt in range(8):
                    mm(psum[:, bst, :], onesb[:], sq[:, mt, :], mt == 0, mt == 7, [R_H[mt], R_const], [R_ps[bst]])
                act(rstd_e[:], psum[:, bst, :], AF.Sqrt, [R_ps[bst]], [R_rstde], scale=1.0 / D, bias=EPS)
                warm(WY)
                P.tag = "Y"
                gel128 = CC[:, 4192:6240].rearrange("p (a l) -> p a l", a=4)
                Y2 = HB[:, 0:4, :].rearrange("p a b -> p (a b)").rearrange("p (t x) -> p t x", t=8)
                R_Y2n = R_H[0:4]
                v4 = "p (g t c) -> p g t c"
                for qq in range(4):
                    b = bank()
                    for ph in range(2):
                        q = 2 * qq + ph
                        for gg in range(4):
                            g = q * 4 + gg
                            half, gp = g // 16, g % 16
                            hs = slice(half * 64, half * 64 + 64)
                            o = psum[ph * 64:(ph + 1) * 64, b, gg * 128:(gg + 1) * 128]
                            rr = R_U + R_Sabf + [R_ops]
                            mm(o, U[:, g, :], RTt[:, g, :], True, False, rr, [R_ps[b]])
                            mm(o, Sabf[hs, gp, :], Qs[hs, gp, 0, :], False, False, rr, [R_ps[b]])
                            mm(o, Sabf[hs, 16 + gp, :], Qs[hs, gp, 1, :], False, True, rr, [R_ps[b]])
                    pv = psum[:, b, :]
                    st_ = qq % 2
                    g1, g2 = gel128[:, 2 * st_, :], gel128[:, 2 * st_ + 1, :]
                    rg = [R_gel[st_]]
                    act(g1, pv, AF.Square, [R_ps[b]] + rg, rg)
                    stt(g1, g1, 1.0 / 0.044715, pv, ALU.add, ALU.mult, [R_ps[b]] + rg, rg)
                    act(g2, g1, AF.Sigmoid, rg, rg, scale=1.5957691216057308 * 0.044715)
                    yo = Y2[:, :, qq * 64:(qq + 1) * 64].rearrange("p t (g c) -> p g t c", g=4)
                    tt("dve", yo, pv.rearrange(v4, g=4, t=8), g2.rearrange(v4, g=4, t=8), ALU.mult, [R_ps[b]] + rg, R_Y2n)
                P.tag = "yT"
                P.add("dve", lambda e: e.reciprocal(out=rstd_e[:], in_=rstd_e[:]), reads=[R_rstde], writes=[R_rstde])
                warm(WT)
                for kc in range(4):
                    b = bank()
                    for t in range(8):
                        for ph in range(2):
                            hp = slice(ph * 64, ph * 64 + 64)
                            tr(psb[hp, b, t * 64:(t + 1) * 64], Y2[hp, t, kc * 64:(kc + 1) * 64], identb[hp, hp],
                               R_Y2n + [R_const], [R_ps[b]])
                    copy_op("act", ysb[:, kc, :].rearrange("p (c t) -> p t c", t=8),
                            psb[:, b, 0:512].rearrange("p (t c) -> p t c", t=8), [R_ps[b]], [R_ysb[kc]])
                P.tag = "pooltail"
                class _V:
                    pass
                for gi in range(4):
                    b = bank()
                    mm(psum[:, b, :], slg[:, 2048 + gi * 128:2048 + (gi + 1) * 128], pooled[:, gi, :], True, True, [rsg, R_pooled], [R_ps[b]])
                    act(yT[:, 4 + gi, :], psum[:, b, :], AF.Copy, [R_ps[b], R_const], [R_yT[4 + gi]], scale=pscw[:, gi:gi + 1])
                    act(sq[:, 4 + gi, :], psum[:, b, :], AF.Square, [R_ps[b], R_const], [R_H[4 + gi]], scale=pscw[:, gi:gi + 1])
                stats_rstd(None, 4, None, 1.0 / 512, presq=(sq[:, 4:8, :], R_H[4:8]), outt=(tC[:], [R_ptmp]))
                for k in range(4):
                    stt(yT[:, 4 + k, :], yT[:, 4 + k, :], cols[:, C_POOLN + k:C_POOLN + k + 1], tC[:], ALU.mult, ALU.mult,
                        [R_yT[4 + k], R_ptmp, R_const], [R_yT[4 + k]])
                P.tag = "GLU"
                for mt in range(4):
                    b = bank()
                    for k in range(4):
                        o = (mt * 4 + k) * 128
                        mm(psum[:, b, :], slg[:, o:o + 128], ysb[:, k, :], k == 0, k == 3, [rsg, R_ysb[k]], [R_ps[b]])
                    gt_, Rg_ = [(gtmp[:], R_gtmp), (silu_t[:, 0, :], R_silu[0]), (silu_t[:, 1, :], R_silu[1])][mt % 3]
                    act(gt_, psum[:, b, :], AF.Sigmoid, [R_ps[b], R_const], [Rg_], bias=cols[:, C_BGLU + mt:C_BGLU + mt + 1])
                    tt("dve", yf[:, mt, :], ysb[:, mt, :], gt_, ALU.mult, [R_ysb[mt], Rg_], R_yf_all if mt == 0 else [R_yfm[mt]])
                    act(sq[:, mt, :], yf[:, mt, :], AF.Square, [R_yfm[mt]], [R_H[mt]])
                stats_rstd(None, 4, None, 1.0 / 512, presq=(sq, R_H))
                for k in range(4):
                    stt(yT[:, k, :], yf[:, k, :], cols[:, C_SSMN + k:C_SSMN + k + 1], rstd[:], ALU.mult, ALU.mult,
                        [R_yfm[k], R_rstd, R_const], [R_yT[k]])
                P.tag = "wout"
                warm(WO)
                for ob in range(2):
                    sl, rs = wload("wout%d" % ob, first)
                    for mt in range(4):
                        b = bank()
                        for k in range(8):
                            o = (mt * 8 + k) * 128
                            mm(psum[:, b, :], sl[:, o:o + 128], yT[:, k, :], k == 0, k == 7, [rs, R_yT[k]], [R_ps[b]])
                        m = ob * 4 + mt
                        tt("dve", hT[:, m, :], hT[:, m, :], psum[:, b, :], ALU.add, [R_ps[b], R_hT[m]], [R_hT[m]])
                        if dbg >= 3:
                            act(sqB[:, m, :], hT[:, m, :], AF.Square, [R_hT[m]], [R_sqB[m]])
            if dbg >= 3:
                P.tag = "F2"
                for m in range(8):
                    tt("pool", e_bf[:, m, :], e_bf[:, m, :], rstd_e[:], ALU.mult, [R_ebf[m], R_rstde], [R_ebf[m]])
                norm_gain_only(C_FFN2, (sqB, R_sqB))
                ffn(2, first, (sqB, R_sqB) if dbg >= 3.1 else None)
            if dbg >= 3.1:
                P.tag = "PLE"
                norm_full(C_GATE, (sqB, R_sqB))
                warm(WG)
                for ob in range(2):
                    sl, rs = wload("gate%d" % ob, first)
                    gb = [bank() for _ in range(4)]
                    if ob == 0:
                        for k in range(8):
                            for mt in range(4):
                                o = (mt * 8 + k) * 128
                                mm(psum[:, gb[mt], :], sl[:, o:o + 128], hn[:, k, :], k == 0, k == 7, [rs, R_hn[k]], [R_ps[gb[mt]]])
                    else:
                        for mt in range(4):
                            for k in range(8):
                                o = (mt * 8 + k) * 128
                                mm(psum[:, gb[mt], :], sl[:, o:o + 128], hn[:, k, :], k == 0, k == 7, [rs, R_hn[k]], [R_ps[gb[mt]]])
                    for mt in range(4):
                        b = gb[mt]
                        m = ob * 4 + mt
                        gt_, Rg_ = [(gtmp[:], R_gtmp), (silu_t[:, 0, :], R_silu[0]), (silu_t[:, 1, :], R_silu[1])][m % 3]
                        act(gt_, psum[:, b, :], AF.Sigmoid, [R_ps[b]], [Rg_])
                        tt("dve", gt_, gt_, e_bf[:, m, :], ALU.mult, [Rg_, R_ebf[m]], [Rg_])
                        tt("dve", ebuf[:, m, :], hT[:, m, :], gt_, ALU.add, [Rg_, R_hT[m]], [R_B[m]])
                if w + 1 < n_win:
                    load_x_piece(w + 1, 0)
                    load_x_piece(w + 1, 1)
            P.tag = "fin"
            if dbg < 3.1:
                for k in range(8):
                    copy_op("act", ebuf[:, k, :], hT[:, k, :], [R_hT[k]], [R_B[k]])
            R_facc = Res("facc")
            for bq in range(4):
                bks = []
                for hb in range(2):
                    b = bank()
                    for kk in range(4):
                        k = hb * 4 + kk
                        tr(psum[:, b, kk * 128:(kk + 1) * 128], ebuf[:, k, bq * 128:(bq + 1) * 128], ident[:], [R_B[k], R_const], [R_ps[b]])
                    P.add("act", lambda e, b=b, j=2 * bq + hb: e.activation(out=gtmp[:], in_=psum[:, b, :], func=AF.Square,
                                                                           accum_out=facc[:, j:j + 1]),
                          reads=[R_ps[b]], writes=[R_gtmp, R_facc])
                    bks.append(b)
                c0 = 8 + bq
                tt("dve", facc[:, c0:c0 + 1], facc[:, 2 * bq:2 * bq + 1], facc[:, 2 * bq + 1:2 * bq + 2], ALU.add, [R_facc], [R_facc])
                act(facc[:, c0:c0 + 1], facc[:, c0:c0 + 1], AF.Sqrt, [R_facc], [R_facc], scale=1.0 / D, bias=EPS)
                P.add("dve", lambda e, c0=c0: e.reciprocal(out=facc[:, c0 + 4:c0 + 5], in_=facc[:, c0:c0 + 1]), reads=[R_facc], writes=[R_facc])
                for hb in range(2):
                    b = bks[hb]
                    stt(otok[:, bq % 2, hb * 512:(hb + 1) * 512], psum[:, b, :], facc[:, c0 + 4:c0 + 5], fnrow[:, hb * 512:(hb + 1) * 512],
                        ALU.mult, ALU.mult, [R_ps[b], R_facc, R_const], [R_otok[bq % 2]])
                t0 = t0w + bq * 128
                op = P.add("act", lambda e, t0=t0, bq=bq: e.dma_start(out=out[t0:t0 + 128, :], in_=otok[:, bq % 2, :]),
                           reads=[R_otok[bq % 2]], dma=True)
                final_ops.append(op)

        P.emit(final_waits=final_ops)
    return nc


def _host_layout(inputs):
    f32 = np.float32

    def colv(v):
        v = np.asarray(v, f32).reshape(-1)
        return np.ascontiguousarray(v.reshape(-1, 128).T)

    cols = np.zeros((128, 64), f32)
    cols[:, 0:8] = colv(inputs["ffn1_norm"][0])
    cols[:, 8:16] = colv(inputs["mix_norm"][0])
    cols[:, 16:24] = colv(inputs["ffn2_norm"][0])
    cols[:, 24:32] = colv(inputs["ple_gate_norm"][0])
    cols[:, 32:40] = colv(inputs["ple_norm"][0])
    cols[:, 40:48] = colv(inputs["final_norm"])
    cols[:, 48:52] = colv(inputs["ssm_out_norm"][0])
    cols[:, 52:56] = colv(inputs["pool_out_norm"][0])
    cols[:, 56:60] = colv(inputs["pool_scale"][0])
    cols[:, 60:64] = colv(inputs["b_glu"][0])
    dsk = np.asarray(inputs["d_skip"][0], f32).reshape(32, 16)
    dcol = np.ascontiguousarray(np.broadcast_to(dsk.T[None, :, :], (8, 16, 32)).reshape(128, 32))

    def gn(a):
        a = np.asarray(a, f32).reshape(2, 16, 64)
        return np.ascontiguousarray(a.transpose(0, 2, 1).reshape(128, 16))

    s5a = np.zeros((128, 3, 16), f32)
    s5a[:, 0] = gn(inputs["a_re"][0])
    s5a[:, 1] = gn(inputs["a_im"][0])
    ld = np.asarray(inputs["log_dt"][0], f32).reshape(2, 1, 16)
    s5a[:, 2] = np.broadcast_to(ld, (2, 64, 16)).reshape(128, 16)

    def gnb(a):
        a = np.asarray(a, f32).reshape(2, 16, 64, 16)
        return np.ascontiguousarray(a.transpose(0, 2, 1, 3).reshape(128, 16, 16))

    def gnc(a):
        a = np.asarray(a, f32).reshape(2, 16, 16, 64)
        return np.ascontiguousarray(a.transpose(0, 3, 1, 2).reshape(128, 16, 16))

    s5b = np.stack([gnb(inputs["b_re"][0]), gnb(inputs["b_im"][0])], axis=1)
    s5c = np.stack([gnc(inputs["c_re"][0]), gnc(inputs["c_im"][0])], axis=1)
    ident = np.eye(128, dtype=f32)
    sidx = np.arange(128) // 16
    cmask = (sidx[None, :] >= sidx[:, None]).astype(f32)
    pcorr = np.zeros((128, 4, 16), f32)
    for gi, wdw in enumerate((2, 4, 8, 16)):
        t = np.arange(16)
        pcorr[:, gi, :] = (wdw / np.minimum(t + 1, wdw)).astype(f32)[None, :]
    common = {
        "cols": cols, "dcol": dcol, "s5a": s5a, "s5b": np.ascontiguousarray(s5b), "s5c": np.ascontiguousarray(s5c),
        "ident": ident, "cmask": cmask, "pcorr": pcorr,
        "fnrow": np.ascontiguousarray(np.broadcast_to(np.asarray(inputs["final_norm"], f32).reshape(1, D), (128, D))),
    }
    wts = {
        "ffn1_w1": inputs["ffn1_w1"], "ffn1_w3": inputs["ffn1_w3"], "ffn1_w2": inputs["ffn1_w2"],
        "ffn2_w1": inputs["ffn2_w1"], "ffn2_w3": inputs["ffn2_w3"], "ffn2_w2": inputs["ffn2_w2"],
        "w_in": inputs["w_in"], "w_out": inputs["w_out"], "w_glu": inputs["w_glu"], "w_pool": inputs["w_pool"],
        "w_ple_gate": inputs["w_ple_gate"], "w_ple_proj": inputs["w_ple_proj"],
    }
    for nm, v in wts.items():
        common[nm] = np.ascontiguousarray(np.asarray(v[0], f32))
    return common


_NC_CACHE = {}


def kernel(**inputs):
    common = _host_layout(inputs)
    x = np.asarray(inputs["x"], np.float32)
    p = np.asarray(inputs["p"], np.float32)[0]
    in_maps = []
    for b in range(8):
        m = dict(common)
        m["x"] = np.ascontiguousarray(x[b])
        m["p"] = np.ascontiguousarray(p[b])
        in_maps.append(m)
    nc = build_program()
    res = run_bass_kernel_spmd(nc, in_maps, core_ids=list(range(8)))
    outs = [np.asarray(r["out"], np.float32).reshape(SEQ, D) for r in res.results]
    return np.stack(outs, axis=0)
```

```python
import math
from contextlib import ExitStack

import numpy as np
import ml_dtypes
import concourse.bass as bass
import concourse.mybir as mybir
from concourse.bass_utils import run_bass_kernel_spmd

F32 = mybir.dt.float32
BF16 = mybir.dt.bfloat16
I32 = mybir.dt.int32
AF = mybir.ActivationFunctionType
ALU = mybir.AluOpType

D = 1024
SEQ = 4096
W = 512
NWIN = SEQ // W
DFF = 2816
NM = DFF // 128
EPS = 1e-6
NSLOT = 4
WF1, WMIX, WF2, WG, WFIN, WY, WT, WO = 12, 8, 35, 30, 25, 30, 20, 25
SLOT = 4096


class Res:
    __slots__ = ("name", "last_write", "readers", "co")

    def __init__(self, name=""):
        self.name = name
        self.last_write = None
        self.readers = []
        self.co = []


class Op:
    __slots__ = ("eng", "fn", "deps", "tick", "has_dep", "is_dma", "sem", "semval", "tag")


class Prog:
    ENGS = ("pe", "act", "dve", "pool", "sp")

    def __init__(self, nc, n_dma_sems=10):
        self.nc = nc
        self.ops = {e: [] for e in self.ENGS}
        self.n_dma_sems = n_dma_sems
        self.dma_rr = {e: 0 for e in self.ENGS}
        self.dma_cnt = {}
        self.dma_last = {}
        self.tag = ""

    def add(self, eng, fn, reads=(), writes=(), dma=False, join=False):
        op = Op()
        op.eng = eng
        op.fn = fn
        op.is_dma = dma
        op.has_dep = False
        op.tag = self.tag
        deps = []
        for r in reads:
            if r.last_write is not None:
                deps.append(r.last_write)
            deps.extend(r.co)
        for w in writes:
            if not join:
                if w.last_write is not None:
                    deps.append(w.last_write)
                deps.extend(w.co)
            deps.extend(w.readers)
        seen = set()
        dl = []
        for d in deps:
            if id(d) in seen:
                continue
            seen.add(id(d))
            if d.eng == "pe" and eng == "pe" and not d.is_dma and not dma:
                continue
            dl.append(d)
        if dma:
            key = (eng, self.dma_rr[eng] % self.n_dma_sems)
            self.dma_rr[eng] += 1
            prev = self.dma_last.get(key)
            if prev is not None and id(prev) not in seen:
                dl.append(prev)
            self.dma_cnt[key] = self.dma_cnt.get(key, 0) + 1
            op.sem = key
            op.semval = 16 * self.dma_cnt[key]
            self.dma_last[key] = op
        for d in dl:
            d.has_dep = True
        op.deps = dl
        for r in reads:
            r.readers.append(op)
        for w in writes:
            if join:
                w.co.append(op)
            else:
                w.last_write = op
                w.co = []
            w.readers = []
        self.ops[eng].append(op)
        return op

    def emit(self, final_waits=()):
        nc = self.nc
        with ExitStack() as st:
            esem = {e: st.enter_context(nc.semaphore("s_" + e)) for e in self.ENGS}
            dsem = {}
            for key in self.dma_cnt:
                dsem[key] = st.enter_context(nc.semaphore("d_%s_%d" % key))
            for e in self.ENGS:
                t = 0
                for op in self.ops[e]:
                    if not op.is_dma and op.has_dep:
                        t += 1
                        op.tick = t
                    else:
                        op.tick = None
            block = st.enter_context(nc.Block())
            prog = self

            def run_engine(ename, eobj):
                waited = {}
                for op in prog.ops[ename]:
                    need = {}
                    for d in op.deps:
                        if d.is_dma:
                            s, v = dsem[d.sem], d.semval
                        else:
                            s, v = esem[d.eng], d.tick
                        k = id(s)
                        if k not in need or need[k][1] < v:
                            need[k] = (s, v)
                    for k, (s, v) in need.items():
                        if waited.get(k, 0) >= v:
                            continue
                        waited[k] = v
                        eobj.wait_ge(s, v)
                    ins = op.fn(eobj)
                    if op.is_dma:
                        ins.then_inc(dsem[op.sem], 16)
                    elif op.has_dep:
                        ins.then_inc(esem[ename], 1)
                if ename == "sp":
                    for op in final_waits:
                        eobj.wait_ge(dsem[op.sem], op.semval)

            @block.tensor
            def _(e):
                run_engine("pe", e)

            @block.scalar
            def _(e):
                run_engine("act", e)

            @block.vector
            def _(e):
                run_engine("dve", e)

            @block.gpsimd
            def _(e):
                run_engine("pool", e)

            @block.sync
            def _(e):
                run_engine("sp", e)


def _kmc(ap, k):
    return ap.rearrange("(k p) (m c) -> p m k c", p=128, c=128)


def build_program(n_win=NWIN, dbg=99):
    nc = bass.Bass("TRN2", target_bir_lowering=False)

    def din(name, shape, dt=F32):
        return nc.dram_tensor(name, list(shape), dt, kind="ExternalInput").ap()

    x = din("x", [SEQ, D])
    pin = din("p", [SEQ, 256])
    wd = {}
    for f in (1, 2):
        wd["ffn%d_w1" % f] = din("ffn%d_w1" % f, [D, DFF])
        wd["ffn%d_w3" % f] = din("ffn%d_w3" % f, [D, DFF])
        wd["ffn%d_w2" % f] = din("ffn%d_w2" % f, [DFF, D])
    wd["w_in"] = din("w_in", [D, D])
    wd["w_out"] = din("w_out", [D, D])
    wd["w_glu"] = din("w_glu", [512, 512])
    wd["w_pool"] = din("w_pool", [4, 128, 128])
    wd["w_ple_gate"] = din("w_ple_gate", [D, D])
    wd["w_ple_proj"] = din("w_ple_proj", [256, D])
    cols_d = din("cols", [128, 64])
    dcol_d = din("dcol", [128, 32])
    s5a_d = din("s5a", [128, 3, 16])
    s5b_d = din("s5b", [128, 2, 16, 16])
    s5c_d = din("s5c", [128, 2, 16, 16])
    ident_d = din("ident", [128, 128])
    cmask_d = din("cmask", [128, 128])
    pcorr_d = din("pcorr", [128, 4, 16])
    fnrow_d = din("fnrow", [128, D])
    out = nc.dram_tensor("out", [SEQ, D], F32, kind="ExternalOutput").ap()

    blocks = {}
    order = []

    def addblk(name, halves):
        blocks[name] = halves
        order.append(name)

    for f in (1, 2):
        w1v = _kmc(wd["ffn%d_w1" % f], 8)
        w3v = _kmc(wd["ffn%d_w3" % f], 8)
        w2v = _kmc(wd["ffn%d_w2" % f], 22)
        for j in range(NM // 2):
            halves = []
            for mm in range(2):
                pcs = []
                for which, wv in enumerate((w1v, w3v)):
                    pcs.append((((mm * 2 + which) * 8) * 128, 8, 128, wv[:, 2 * j + mm, :, :]))
                halves.append(pcs)
            addblk("f%d_up%d" % (f, j), halves)
        for dt_ in range(8):
            addblk("f%d_dn%d" % (f, dt_), [
                [(0, 16, 128, w2v[:, dt_, 0:16, :])],
                [(2048, 6, 128, w2v[:, dt_, 16:22, :])],
            ])
    winv = _kmc(wd["w_in"], 8)
    addblk("win_pool", [
        [((mt * 8) * 128, 8, 128, winv[:, 4 + mt, :, :]) for mt in (0, 1)],
        [((mt * 8) * 128, 8, 128, winv[:, 4 + mt, :, :]) for mt in (2, 3)],
    ])
    winr = wd["w_in"].rearrange("(k p) n -> p k n", p=128)
    addblk("win_ssm", [
        [(0, 4, 512, winr[:, 0:4, 0:512])],
        [(2048, 4, 512, winr[:, 4:8, 0:512])],
    ])
    for nm, key in (("wout", "w_out"), ("gate", "w_ple_gate")):
        wv = _kmc(wd[key], 8)
        for ob in range(2):
            addblk("%s%d" % (nm, ob), [
                [((mt * 8) * 128, 8, 128, wv[:, 4 * ob + mt, :, :]) for mt in (0, 1)],
                [((mt * 8) * 128, 8, 128, wv[:, 4 * ob + mt, :, :]) for mt in (2, 3)],
            ])
    gluv = _kmc(wd["w_glu"], 4)
    addblk("glupool", [
        [((mt * 4) * 128, 4, 128, gluv[:, mt, :, :]) for mt in range(4)],
        [(2048, 4, 128, wd["w_pool"].rearrange("g p d -> p g d"))],
    ])
    prv = _kmc(wd["w_ple_proj"], 2)
    addblk("proj", [
        [((mt * 2) * 128, 2, 128, prv[:, mt, :, :]) for mt in range(8)],
    ])
    blk_idx = {n: i for i, n in enumerate(order)}
    NB = len(order)
    scratch = nc.dram_tensor("wscratch", [NB, 128, SLOT], BF16, kind="Internal").ap()

    with ExitStack() as st:
        def sb(name, shape, dt):
            return st.enter_context(nc.sbuf_tensor("sb_" + name, list(shape), dt))

        hT = sb("hT", [128, 8, W], F32)
        hn = sb("hn", [128, 8, W], BF16)
        yT = sb("yT", [128, 8, W], BF16)
        PT = sb("PT", [128, 32, 128], BF16)
        RTt = sb("RT", [128, 32, 128], BF16)
        Qs = sb("Qs", [128, 16, 2, 128], BF16)
        wslots = sb("wslots", [128, NSLOT, SLOT], BF16)
        Etab = sb("Etab", [128, 2, 64, 16], F32)
        rho = sb("rho", [128, 16], F32)
        dummy = sb("dummy", [128, 2], F32)
        xtok = sb("xtok", [128, 2, D], F32)
        otok = sb("otok", [128, 2, D], F32)
        ptok = sb("ptok", [128, 4, 256], F32)
        pT = sb("pT", [128, 2, W], BF16)
        silu_t = sb("silu_t", [128, 2, W], F32)
        rstd = sb("rstd", [128, W], F32)
        gtmp = sb("gtmp", [128, W], F32)
        HB = sb("HB", [128, 28, W], BF16)
        BB = sb("BB", [128, 8, W], F32)
        CC = sb("CC", [128, 6240], F32)
        cols = sb("cols", [128, 64], F32)
        dcol = sb("dcol", [128, 32], F32)
        rstd_e = sb("rstd_e", [128, W], F32)
        pscw = sb("pscw", [128, 4], F32)
        fnrow = sb("fnrow", [128, D], F32)
        facc = sb("facc", [128, 16], F32)
        tC = sb("tC", [128, 512], F32)
        ident = sb("ident", [128, 128], F32)
        identb = sb("identb", [128, 128], BF16)
        onesb = sb("onesb", [128, 128], BF16)
        warmt = sb("warmt", [128, W], BF16)
        cmask = sb("cmask", [128, 128], F32)
        pcorr = sb("pcorr", [128, 4, 16], F32)
        s5a = sb("s5a", [128, 3, 16], F32)
        HBf = HB[:].rearrange("p a b -> p (a b)").bitcast(F32)
        s5b = HBf[:, 0:512].rearrange("p (a g c) -> p a g c", a=2, g=16)
        s5c = HBf[:, 512:1024].rearrange("p (a g c) -> p a g c", a=2, g=16)
        sm = HBf[:, 1024:1664].rearrange("p (a c) -> p a c", a=40)
        smi = HBf[:, 1664:1680].bitcast(I32)
        pw = HBf[:, 1696:1984].rearrange("p (k r g) -> p k r g", k=9, r=2)
        bbar = HBf[:, 2048:2560].rearrange("p (a g c) -> p a g c", a=2, g=16)
        Wc = sb("Wc", [128, 32], F32)
        psum = st.enter_context(nc.psum_tensor("psum", [128, 8, W], F32))

        P = Prog(nc)

        R_hT = [Res("hT%d" % k) for k in range(8)]
        R_hn = [Res("hn%d" % k) for k in range(8)]
        R_yT = [Res("yT%d" % k) for k in range(8)]
        R_H = [Res("H%d" % k) for k in range(28)]
        R_B = [Res("B%d" % k) for k in range(8)]
        R_ps = [Res("ps%d" % k) for k in range(8)]
        R_slot = [Res("slot%d" % k) for k in range(NSLOT)]
        R_stage = []
        R_xtok = [Res("xtok0"), Res("xtok1")]
        R_otok = [Res("otok0"), Res("otok1")]
        R_scr = [Res("scr%d" % k) for k in range(NB)]
        R_ops = Res("s5ops")
        R_const = Res("const")
        R_sm = Res("sm")
        R_ptok = Res("ptok")
        R_pT = Res("pT")
        R_silu = [Res("silu0"), Res("silu1")]
        R_rstd = Res("rstd")
        R_gtmp = Res("gtmp")
        R_vpool = Res("vpool")
        R_ptmp = Res("ptmp")
        R_yf = Res("yf")
        R_state = Res("state")
        R_st = Res("st")
        R_out = Res("out")

        hid = HB
        sq = HB
        Z2f = HB[0:64, 8:16, :].rearrange("p a b -> p (a b)")
        Z2g = Z2f.rearrange("p (g s c) -> p g s c", g=32, s=8)
        Y2bf = HB[0:64, 0:8, :]
        R_Y2 = R_H[0:8]
        U = HB[:, 16:20, :].rearrange("p a b -> p (a b)").rearrange("p (g c) -> p g c", g=32)
        Sabf = HB[:, 20:24, :].rearrange("p a b -> p (a b)").rearrange("p (r c) -> p r c", r=32)
        ysb = HB[:, 24:28, :]
        R_Z2 = R_H[8:16]
        R_U = R_H[16:20]
        R_Sabf = R_H[20:24]
        R_ysb = R_H[24:28]
        BBf = BB[:].rearrange("p a b -> p (a b)")
        VV = BBf[:, 0:2048].rearrange("p (c r) -> p c r", r=32)
        ebuf = BB
        vpool = CC[:, 0:2112].rearrange("p (g l) -> p g l", g=4)
        tA = CC[:, 2112:2640]
        tB = CC[:, 2640:3168]
        pooled = CC[:, 3168:4192].bitcast(BF16).rearrange("p (g l) -> p g l", g=4)
        gel = CC[0:64, 4192:6240].rearrange("p (a l) -> p a l", a=4)
        yf = CC[:, 4192:6240].rearrange("p (g l) -> p g l", g=4)

        sqA = yT
        R_sqA = R_yT
        sqB = otok[:].rearrange("p a b -> p (a b)").bitcast(BF16).rearrange("p (k t) -> p k t", k=8)
        R_sqB = [R_otok[0]] * 4 + [R_otok[1]] * 4
        e_bf = xtok[:].rearrange("p a b -> p (a b)").bitcast(BF16).rearrange("p (k t) -> p k t", k=8)
        R_ebf = [R_xtok[0]] * 4 + [R_xtok[1]] * 4
        R_rstde = Res("rstd_e")
        R_gel = [Res("gel0"), Res("gel1")]
        R_ptmp_all = [R_ptmp]
        R_pooled = Res("pooled")
        R_yfm = [Res("yfm%d" % k) for k in range(4)]
        R_yf_all = [R_yf] + R_gel + R_yfm

        bank_rr = [0]

        def bank():
            b = bank_rr[0] % 7
            bank_rr[0] += 1
            return b

        flip = [0]

        def evac_eng():
            flip[0] += 1
            return "act" if flip[0] % 2 else "dve"

        def warm(n):
            for _ in range(n):
                P.add("pe", lambda e: e.matmul(psum[:, 7, :], lhsT=onesb[:], rhs=warmt[:], start=True, stop=True))

        def copy_op(eng, out_ap, in_ap, reads, writes):
            if eng == "act":
                P.add("act", lambda e: e.copy(out=out_ap, in_=in_ap), reads=reads, writes=writes)
            else:
                P.add(eng, lambda e: e.tensor_copy(out=out_ap, in_=in_ap), reads=reads, writes=writes)

        def tt(eng, out_ap, a, b, op, reads, writes):
            P.add(eng, lambda e: e.tensor_tensor(out=out_ap, in0=a, in1=b, op=op), reads=reads, writes=writes)

        def ts(eng, out_ap, a, s1, s2, op0, op1, reads, writes):
            if s2 is None:
                P.add(eng, lambda e: e.tensor_scalar(out=out_ap, in0=a, scalar1=s1, scalar2=None, op0=op0), reads=reads, writes=writes)
            else:
                P.add(eng, lambda e: e.tensor_scalar(out=out_ap, in0=a, scalar1=s1, scalar2=s2, op0=op0, op1=op1), reads=reads, writes=writes)

        def stt(out_ap, a, scalar, b, op0, op1, reads, writes):
            P.add("dve", lambda e: e.scalar_tensor_tensor(out=out_ap, in0=a, scalar=scalar, in1=b, op0=op0, op1=op1), reads=reads, writes=writes)

        def act(out_ap, in_ap, func, reads, writes, scale=1.0, bias=0.0):
            P.add("act", lambda e: e.activation(out=out_ap, in_=in_ap, func=func, bias=bias, scale=scale), reads=reads, writes=writes)

        def mm(out_ap, lhsT, rhs, start, stop, reads, writes):
            P.add("pe", lambda e: e.matmul(out_ap, lhsT=lhsT, rhs=rhs, start=start, stop=stop), reads=reads, writes=writes)

        def tr(out_ap, in_ap, idn, reads, writes):
            P.add("pe", lambda e: e.transpose(out_ap, in_ap, idn), reads=reads, writes=writes)

        for dst, src in ((cols[:], cols_d), (dcol[:], dcol_d), (ident[:], ident_d), (cmask[:], cmask_d),
                         (pcorr[:], pcorr_d), (fnrow[:], fnrow_d), (s5a[:], s5a_d), (s5b, s5b_d), (s5c, s5c_d)):
            P.add("sp", lambda e, dst=dst, src=src: e.dma_start(out=dst, in_=src), writes=[R_const], dma=True, join=True)
        P.add("dve", lambda e: e.tensor_copy(out=identb[:], in_=ident[:]), reads=[R_const], writes=[R_const])
        P.add("dve", lambda e: e.memset(onesb[:], 1.0), writes=[R_const])
        P.add("dve", lambda e: e.memset(warmt[:], 1.0), writes=[R_const])
        for gi_, wdw_ in enumerate((2, 4, 8, 16)):
            P.add("dve", lambda e, gi_=gi_, wdw_=wdw_: e.tensor_scalar(out=pscw[:, gi_:gi_ + 1], in0=cols[:, 56 + gi_:57 + gi_],
                                                                      scalar1=1.0 / wdw_, scalar2=None, op0=ALU.mult),
                  reads=[R_const], writes=[R_const])
        P.add("pool", lambda e: e.memset(Wc[:], 0.0), writes=[R_state])

        C_FFN1, C_MIX, C_FFN2, C_GATE, C_PLE, C_FIN = 0, 8, 16, 24, 32, 40
        C_SSMN, C_POOLN, C_PSCALE, C_BGLU = 48, 52, 56, 60

        P.tag = "pro"
        def S(i):
            return sm[:, i, :]

        rc = [R_const, R_sm]

        def v_tt(o, a, b, op):
            tt("dve", o, a, b, op, rc, [R_sm])

        def v_ts(o, a, s1, s2, op0, op1=None):
            ts("dve", o, a, s1, s2, op0, op1, rc, [R_sm])

        are, aim, ldt = s5a[:, 0, :], s5a[:, 1, :], s5a[:, 2, :]
        act(S(0), ldt, AF.Exp, rc, [R_sm])
        v_tt(S(1), are, S(0), ALU.mult)
        v_tt(S(2), aim, S(0), ALU.mult)
        act(S(3), S(1), AF.Exp, rc, [R_sm])

        def sincos(dst, src_ang, extra):
            v_ts(S(30), src_ang, 1.0 / (2 * math.pi), 8.5 + extra, ALU.mult, ALU.add)
            P.add("dve", lambda e: e.tensor_copy(out=smi, in_=S(30)), reads=rc, writes=[R_sm])
            P.add("dve", lambda e: e.tensor_copy(out=S(31), in_=smi), reads=rc, writes=[R_sm])
            v_tt(S(30), S(30), S(31), ALU.subtract)
            v_ts(S(31), S(30), 0.0, None, ALU.is_lt)
            v_tt(S(30), S(30), S(31), ALU.add)
            act(dst, S(30), AF.Sin, rc, [R_sm], scale=2 * math.pi, bias=-math.pi)

        sincos(S(4), S(2), 0.0)
        sincos(S(5), S(2), 0.25)
        a_r, a_i = pw[:, 1, 0, :], pw[:, 1, 1, :]
        v_tt(a_r, S(3), S(5), ALU.mult)
        v_tt(a_i, S(3), S(4), ALU.mult)
        P.add("dve", lambda e: e.memset(pw[:, 0, 0, :], 1.0), reads=rc, writes=[R_sm])
        P.add("dve", lambda e: e.memset(pw[:, 0, 1, :], 0.0), reads=rc, writes=[R_sm])
        for k in range(2, 9):
            pr, pi_, qr, qi = pw[:, k - 1, 0, :], pw[:, k - 1, 1, :], pw[:, k, 0, :], pw[:, k, 1, :]
            v_tt(S(6), pr, a_r, ALU.mult)
            v_tt(S(7), pi_, a_i, ALU.mult)
            v_tt(qr, S(6), S(7), ALU.subtract)
            v_tt(S(6), pr, a_i, ALU.mult)
            v_tt(S(7), pi_, a_r, ALU.mult)
            v_tt(qi, S(6), S(7), ALU.add)
        v_tt(S(6), are, are, ALU.mult)
        v_tt(S(7), aim, aim, ALU.mult)
        v_tt(S(6), S(6), S(7), ALU.add)
        P.add("dve", lambda e: e.reciprocal(out=S(8), in_=S(6)), reads=rc, writes=[R_sm])
        v_ts(S(9), a_r, -1.0, None, ALU.add)
        v_tt(S(6), S(9), are, ALU.mult)
        v_tt(S(7), a_i, aim, ALU.mult)
        v_tt(S(6), S(6), S(7), ALU.add)
        v_tt(S(10), S(6), S(8), ALU.mult)
        v_tt(S(6), a_i, are, ALU.mult)
        v_tt(S(7), S(9), aim, ALU.mult)
        v_tt(S(6), S(6), S(7), ALU.subtract)
        v_tt(S(11), S(6), S(8), ALU.mult)
        p8r, p8i = pw[:, 8, 0, :], pw[:, 8, 1, :]
        v_tt(S(6), p8r, p8r, ALU.mult)
        v_tt(S(7), p8i, p8i, ALU.mult)
        v_tt(S(6), S(6), S(7), ALU.add)
        P.add("dve", lambda e: e.reciprocal(out=S(7), in_=S(6)), reads=rc, writes=[R_sm])
        v_tt(S(12), p8r, S(7), ALU.mult)
        v_tt(S(13), p8i, S(7), ALU.mult)
        v_ts(S(13), S(13), -1.0, None, ALU.mult)
        act(rho[:], S(1), AF.Exp, rc, [R_sm], scale=8.0)
        act(S(32), S(1), AF.Exp, rc, [R_sm], scale=-8.0)
        v_tt(Etab[:, 0, 0, :], p8r, S(32), ALU.mult)
        v_tt(S(33), p8i, S(32), ALU.mult)
        v_ts(Etab[:, 1, 0, :], S(33), -1.0, None, ALU.mult)
        TE = HBf[:, 6656:7168].rearrange("p (c g) -> p c g", g=16)
        for L in (1, 2, 4, 8, 16, 32):
            er0, ei0 = Etab[:, 0, 0:L, :], Etab[:, 1, 0:L, :]
            br = Etab[:, 0, L - 1, :].unsqueeze(1).to_broadcast([128, L, 16])
            bi = Etab[:, 1, L - 1, :].unsqueeze(1).to_broadcast([128, L, 16])
            dr, di = Etab[:, 0, L:2 * L, :], Etab[:, 1, L:2 * L, :]
            tmpE = TE[:, 0:L, :]
            v_tt(dr, er0, br, ALU.mult)
            v_tt(tmpE, ei0, bi, ALU.mult)
            v_tt(dr, dr, tmpE, ALU.subtract)
            v_tt(di, er0, bi, ALU.mult)
            v_tt(tmpE, ei0, br, ALU.mult)
            v_tt(di, di, tmpE, ALU.add)

        def bc16(ap2):
            return ap2.unsqueeze(2).to_broadcast([128, 16, 16])

        T0 = sm[:, 14:30, :]
        bre, bim = s5b[:, 0, :, :], s5b[:, 1, :, :]
        bbr, bbi = bbar[:, 0, :, :], bbar[:, 1, :, :]
        v_tt(bbr, bre, bc16(S(10)), ALU.mult)
        v_tt(T0, bim, bc16(S(11)), ALU.mult)
        v_tt(bbr, bbr, T0, ALU.subtract)
        v_tt(bbi, bim, bc16(S(10)), ALU.mult)
        v_tt(T0, bre, bc16(S(11)), ALU.mult)
        v_tt(bbi, bbi, T0, ALU.add)

        def big(ap2):
            return ap2.rearrange("p (g x) -> p g x", g=16)

        Pr = big(BBf[:, 0:2048])
        Pi = big(BBf[:, 2048:4096])
        P2r = big(CC[:, 0:2048])
        P2i = big(CC[:, 2048:4096])
        Qr = big(HBf[:, 2560:4608])
        Qni = big(HBf[:, 4608:6656])
        Tb = big(CC[:, 4096:6144])
        R_pro = R_B + [R_vpool, R_ptmp, R_gel[0], R_gel[1], R_yf] + [R_sm, R_const]

        def b_tt(o, a, b, op):
            tt("dve", o, a, b, op, R_pro, R_pro)

        T1 = sm[:, 14:30, :]
        for s in range(8):
            k = 7 - s
            o_r = Pr[:, :, s * 16:(s + 1) * 16]
            o_i = Pi[:, :, s * 16:(s + 1) * 16]
            wr, wi = bc16(pw[:, k, 0, :]), bc16(pw[:, k, 1, :])
            b_tt(o_r, bbr, wr, ALU.mult)
            b_tt(T1, bbi, wi, ALU.mult)
            b_tt(o_r, o_r, T1, ALU.subtract)
            b_tt(o_i, bbi, wr, ALU.mult)
            b_tt(T1, bbr, wi, ALU.mult)
            b_tt(o_i, o_i, T1, ALU.add)
        for t in range(8):
            k = t + 1
            o_r = Qr[:, :, t * 16:(t + 1) * 16]
            o_i = Qni[:, :, t * 16:(t + 1) * 16]
            wr, wi = bc16(pw[:, k, 0, :]), bc16(pw[:, k, 1, :])
            cre, cim = s5c[:, 0, :, :], s5c[:, 1, :, :]
            b_tt(o_r, cre, wr, ALU.mult)
            b_tt(T1, cim, wi, ALU.mult)
            b_tt(o_r, o_r, T1, ALU.subtract)
            b_tt(o_i, cre, wi, ALU.mult)
            b_tt(T1, cim, wr, ALU.mult)
            b_tt(o_i, o_i, T1, ALU.add)
        ts("dve", Qni, Qni, -1.0, None, ALU.mult, None, R_pro, R_pro)

        def bc128(ap2):
            return ap2.unsqueeze(2).to_broadcast([128, 16, 128])

        b_tt(P2r, Pr, bc128(S(12)), ALU.mult)
        b_tt(Tb, Pi, bc128(S(13)), ALU.mult)
        b_tt(P2r, P2r, Tb, ALU.subtract)
        b_tt(P2i, Pi, bc128(S(12)), ALU.mult)
        b_tt(Tb, Pr, bc128(S(13)), ALU.mult)
        b_tt(P2i, P2i, Tb, ALU.add)
        P.add("dve", lambda e: e.tensor_copy(out=Qs[:, :, 0, :], in_=Qr), reads=R_pro, writes=[R_ops])
        P.add("dve", lambda e: e.tensor_copy(out=Qs[:, :, 1, :], in_=Qni), reads=R_pro, writes=[R_ops])
        for g in range(32):
            half, gp = g // 16, g % 16
            b = bank()
            hs = slice(half * 64, half * 64 + 64)
            for ri, Px in enumerate((Pr, Pi)):
                tr(psum[:, b, ri * 64:(ri + 1) * 64], Px[hs, gp, :], ident[hs, hs], R_pro, [R_ps[b]])
            pe_ = evac_eng()
            if pe_ == "act":
                P.add("act", lambda e, g=g, b=b: e.copy(out=PT[:, g, :], in_=psum[:, b, 0:128]), reads=[R_ps[b]], writes=[R_ops], join=True)
            else:
                P.add("dve", lambda e, g=g, b=b: e.tensor_copy(out=PT[:, g, :], in_=psum[:, b, 0:128]), reads=[R_ps[b]], writes=[R_ops], join=True)
            b2 = bank()
            mm(psum[:, b2, 0:128], P2r[hs, gp, :], Qr[hs, gp, :], True, False, R_pro, [R_ps[b2]])
            mm(psum[:, b2, 0:128], P2i[hs, gp, :], Qni[hs, gp, :], False, True, R_pro, [R_ps[b2]])
            tt("dve", gtmp[:, 0:128], psum[:, b2, 0:128], cmask[:], ALU.mult, [R_ps[b2], R_const], [R_gtmp])
            P.add("dve", lambda e, g=g: e.scalar_tensor_tensor(out=RTt[:, g, :], in0=ident[:], scalar=dcol[:, g:g + 1], in1=gtmp[:, 0:128],
                                                              op0=ALU.mult, op1=ALU.add),
                  reads=[R_gtmp, R_const], writes=[R_ops], join=True)

        P.add("pool", lambda e: e.memset(vpool[:, :, 0:16], 0.0), reads=R_pro, writes=[R_vpool])
        P.add("dve", lambda e: e.memset(dummy[:], 0.0), reads=[R_ops], writes=[R_sm, R_const, R_ops] + R_H)

        slot_rr = [0]
        cvt_flip = [0]

        def wload(name, first):
            bi = blk_idx[name]
            si = slot_rr[0] % NSLOT
            slot_rr[0] += 1
            sl = wslots[:, si, :]
            if first:
                n = max(pc[0] + pc[1] * pc[2] for pcs in blocks[name] for pc in pcs)
                first_piece = True
                for pcs in blocks[name]:
                    for (off, k, c, src) in pcs:
                        dst = sl[:, off:off + k * c].rearrange("p (k c) -> p k c", k=k)
                        P.add("pool", lambda e, dst=dst, src=src: e.dma_start(out=dst, in_=src),
                              writes=[R_slot[si]], dma=True, join=not first_piece)
                        first_piece = False
                P.add("sp", lambda e, sl=sl, bi=bi, n=n: e.dma_start(out=scratch[bi, :, 0:n], in_=sl[:, 0:n]),
                      reads=[R_slot[si]], writes=[R_scr[bi]], dma=True)
            else:
                n = max(pc[0] + pc[1] * pc[2] for pcs in blocks[name] for pc in pcs)
                P.add("sp", lambda e, sl=sl, bi=bi, n=n: e.dma_start(out=sl[:, 0:n], in_=scratch[bi, :, 0:n]),
                      reads=[R_scr[bi]], writes=[R_slot[si]], dma=True)
            return sl, R_slot[si]

        def stats_rstd(src_chunks, nchunk, reads_src, inv_n, presq=None, outt=None):
            b = bank()
            sqv, Rsq = (sq, R_H) if presq is None else presq
            for k in range(nchunk):
                if presq is None:
                    act(sqv[:, k, :], src_chunks(k), AF.Square, [reads_src[k]], [Rsq[k]])
                mm(psum[:, b, :], onesb[:], sqv[:, k, :], k == 0, k == nchunk - 1, [Rsq[k], R_const], [R_ps[b]])
            ro_, Rro_ = (rstd[:], [R_rstd]) if outt is None else outt
            act(ro_, psum[:, b, :], AF.Sqrt, [R_ps[b]], Rro_, scale=inv_n, bias=EPS)
            P.add("dve", lambda e: e.reciprocal(out=ro_, in_=ro_), reads=Rro_, writes=Rro_)

        def norm_full(gc, presq=None):
            stats_rstd(lambda k: hT[:, k, :], 8, R_hT, 1.0 / D, presq)
            for k in range(8):
                stt(hn[:, k, :], hT[:, k, :], cols[:, gc + k:gc + k + 1], rstd[:], ALU.mult, ALU.mult,
                    [R_hT[k], R_rstd, R_const], [R_hn[k]])

        def norm_gain_only(gc, presq):
            for k in range(8):
                ts("dve", hn[:, k, :], hT[:, k, :], cols[:, gc + k:gc + k + 1], None, ALU.mult, None, [R_hT[k], R_const], [R_hn[k]])
            stats_rstd(lambda k: hT[:, k, :], 8, R_hT, 1.0 / D, presq)

        def ffn(f, first, sqnext=None):
            for j in range(NM // 2):
                sl, rs = wload("f%d_up%d" % (f, j), first)
                bks = [(bank(), bank()) for _ in range(2)]
                if j == 0:
                    for k in range(8):
                        for mm_ in range(2):
                            for which in range(2):
                                o = ((mm_ * 2 + which) * 8 + k) * 128
                                b = bks[mm_][which]
                                mm(psum[:, b, :], sl[:, o:o + 128], hn[:, k, :], k == 0, k == 7, [rs, R_hn[k]], [R_ps[b]])
                else:
                    for mm_ in range(2):
                        for which in range(2):
                            b = bks[mm_][which]
                            for k in range(8):
                                o = ((mm_ * 2 + which) * 8 + k) * 128
                                mm(psum[:, b, :], sl[:, o:o + 128], hn[:, k, :], k == 0, k == 7, [rs, R_hn[k]], [R_ps[b]])
                for mm_ in range(2):
                    m = 2 * j + mm_
                    bA, bB = bks[mm_]
                    st_ = silu_t[:, m % 2, :]
                    Rs_ = [R_silu[m % 2]]
                    tt("dve", st_, psum[:, bA, :], rstd[:], ALU.mult, [R_ps[bA], R_rstd], Rs_)
                    act(st_, st_, AF.Silu, Rs_, Rs_)
                    tt("dve", st_, st_, rstd[:], ALU.mult, Rs_ + [R_rstd], Rs_)
                    tt("dve", hid[:, m, :], st_, psum[:, bB, :], ALU.mult, Rs_ + [R_ps[bB]], [R_H[m]])
            for dt_ in range(8):
                sl, rs = wload("f%d_dn%d" % (f, dt_), first)
                b = bank()
                for k in range(NM):
                    mm(psum[:, b, :], sl[:, k * 128:(k + 1) * 128], hid[:, k, :], k == 0, k == NM - 1, [rs, R_H[k]], [R_ps[b]])
                stt(hT[:, dt_, :], psum[:, b, :], 0.5, hT[:, dt_, :], ALU.mult, ALU.add, [R_ps[b], R_hT[dt_]], [R_hT[dt_]])
                if sqnext is not None:
                    act(sqnext[0][:, dt_, :], hT[:, dt_, :], AF.Square, [R_hT[dt_]], [sqnext[1][dt_]])

        def load_x_piece(w, bq):
            t0 = w * W + bq * 128
            P.add("sp", lambda e: e.dma_start(out=xtok[:, bq % 2, :], in_=x[t0:t0 + 128, :]), writes=[R_xtok[bq % 2]], dma=True)

        final_ops = []

        for w in range(n_win):
            first = (w == 0)
            cpe = "dve" if first else "pool"
            t0w = w * W
            P.tag = "X"
            if w == 0:
                load_x_piece(0, 0)
                load_x_piece(0, 1)
            P.add("sp", lambda e, t0w=t0w: e.dma_start(out=ptok[:], in_=pin[t0w:t0w + W, :].rearrange("(b p) d -> p b d", p=128)),
                  writes=[R_ptok], dma=True)
            for bq in range(4):
                for hb in range(2):
                    b = bank()
                    for kk in range(4):
                        k = hb * 4 + kk
                        tr(psum[:, b, kk * 128:(kk + 1) * 128], xtok[:, bq % 2, k * 128:(k + 1) * 128], ident[:],
                           [R_xtok[bq % 2], R_const], [R_ps[b]])
                    copy_op(evac_eng(), hT[:, hb * 4:hb * 4 + 4, bq * 128:(bq + 1) * 128],
                            psum[:, b, :].rearrange("p (a c) -> p a c", a=4), [R_ps[b]], R_hT[hb * 4:hb * 4 + 4])
                    if dbg >= 1:
                        act(sqA[:, hb * 4:hb * 4 + 4, bq * 128:(bq + 1) * 128], hT[:, hb * 4:hb * 4 + 4, bq * 128:(bq + 1) * 128],
                            AF.Square, R_hT[hb * 4:hb * 4 + 4], R_sqA[hb * 4:hb * 4 + 4])
                if bq + 2 < 4:
                    load_x_piece(w, bq + 2)
            if dbg >= 1:
                P.tag = "F1"
                norm_gain_only(C_FFN1, (sqA, R_sqA))
                ffn(1, first, (sqA, R_sqA) if dbg >= 2 else None)
            if dbg >= 2:
                P.tag = "mixin"
                norm_full(C_MIX, (sqA, R_sqA))
                warm(WMIX)
                sl, rs = wload("win_ssm", first)
                sb4 = [bank() for _ in range(4)]
                for k in range(8):
                    for s in range(4):
                        mm(psum[0:64, sb4[s], :], hn[:, k, s:W:8], sl[:, k * 512:(k + 1) * 512], k == 0, k == 7, [rs, R_hn[k]], [R_ps[sb4[s]]])
                for s in range(8):
                    if s < 4:
                        b = sb4[s]
                    else:
                        b = bank()
                        for k in range(8):
                            mm(psum[0:64, b, :], hn[:, k, s:W:8], sl[:, k * 512:(k + 1) * 512], k == 0, k == 7, [rs, R_hn[k]], [R_ps[b]])
                    copy_op("act", Z2g[:, :, s, :], psum[0:64, b, :].rearrange("p (g c) -> p g c", g=32), [R_ps[b]], R_Z2)
                slg, rsg = wload("glupool", first)
                P.tag = "U"
                psb = psum[:].rearrange("p a b -> p (a b)").bitcast(BF16).rearrange("p (a b) -> p a b", a=8)
                for gb in range(2):
                    b = bank()
                    for gg in range(16):
                        g = gb * 16 + gg
                        tr(psb[:, b, gg * 64:(gg + 1) * 64], Z2f[:, g * 128:(g + 1) * 128], identb[0:64, 0:64],
                           R_Z2 + [R_const], [R_ps[b]])
                    copy_op("act", U[:, gb * 16:(gb + 1) * 16, :], psb[:, b, :].rearrange("p (g c) -> p g c", g=16),
                            [R_ps[b]], R_U)
                P.tag = "V"
                vb = [bank() for _ in range(4)]
                for g in range(32):
                    half, gp = g // 16, g % 16
                    for ri in range(2):
                        col = (ri * 16 + gp) * 64
                        b = vb[col // 512]
                        mm(psum[half * 64:(half + 1) * 64, b, col % 512:col % 512 + 64], PT[:, g, ri * 64:(ri + 1) * 64], U[:, g, :],
                           True, True, R_U + [R_ops], [R_ps[b]])
                for q in range(4):
                    b = vb[q]
                    copy_op("act", VV[:, :, q * 8:(q + 1) * 8].rearrange("p c r -> p r c"),
                            psum[:, b, :].rearrange("p (r c) -> p r c", r=8), [R_ps[b]], R_B)
                P.tag = "scan"
                Wd = BBf[:, 2048:4096].rearrange("p (c r) -> p c r", r=32)
                Er, Ei = Etab[:, 0, :, :], Etab[:, 1, :, :]
                tq1 = CC[:, 4192:5216].rearrange("p (c g) -> p c g", g=16)
                tq2 = CC[:, 5216:6240].rearrange("p (c g) -> p c g", g=16)
                RV, RW = R_B[0:4], R_B[4:8]
                ro = [R_ops]
                tt("dve", Wd[:, :, 0:16], Er, VV[:, :, 0:16], ALU.mult, RV + ro, RW)
                tt("dve", tq1, Ei, VV[:, :, 16:32], ALU.mult, RV + ro, R_yf_all)
                tt("dve", Wd[:, :, 16:32], Er, VV[:, :, 16:32], ALU.mult, RV + ro, RW)
                tt("dve", tq2, Ei, VV[:, :, 0:16], ALU.mult, RV + ro + R_yf_all, R_yf_all)
                tt("dve", Wd[:, :, 0:16], Wd[:, :, 0:16], tq1, ALU.subtract, RW + R_yf_all, RW)
                tt("dve", Wd[:, :, 16:32], Wd[:, :, 16:32], tq2, ALU.add, RW + R_yf_all, RW)
                P.add("dve", lambda e: e.tensor_copy(out=Sabf[:, :, 0], in_=Wc[:]), reads=[R_state], writes=R_Sabf)
                for r in range(32):
                    gp = r % 16
                    P.add("dve", lambda e, r=r, gp=gp: e.tensor_tensor_scan(
                        out=VV[:, :, r], data0=rho[:, gp:gp + 1].to_broadcast([128, 64]), data1=Wd[:, :, r],
                        initial=Wc[:, r:r + 1], op0=ALU.mult, op1=ALU.add),
                        reads=RW + [R_state, R_ops], writes=RV, join=(r > 0))
                tt("dve", Wd[:, :, 0:16], Er, VV[:, :, 0:16], ALU.mult, RV + ro, RW)
                tt("dve", tq1, Ei, VV[:, :, 16:32], ALU.mult, RV + ro + R_yf_all, R_yf_all)
                tt("dve", Wd[:, :, 16:32], Er, VV[:, :, 16:32], ALU.mult, RV + ro, RW)
                tt("dve", tq2, Ei, VV[:, :, 0:16], ALU.mult, RV + ro + R_yf_all, R_yf_all)
                tt("dve", Wd[:, :, 0:16], Wd[:, :, 0:16], tq1, ALU.add, RW + R_yf_all, RW)
                tt("dve", Wd[:, :, 16:32], Wd[:, :, 16:32], tq2, ALU.subtract, RW + R_yf_all, RW)
                P.add("dve", lambda e: e.tensor_copy(out=Sabf[:, :, 1:64], in_=Wd[:, 0:63, :].rearrange("p c r -> p r c")),
                      reads=RW, writes=R_Sabf)
                P.add(cpe, lambda e: e.tensor_copy(out=Wc[:], in_=Wd[:, 63, :]), reads=RW + R_Sabf, writes=[R_state])
                P.tag = "Y"
                P.tag = "zpool"
                sl, rs = wload("win_pool", first)
                zb = [bank() for _ in range(4)]
                for k in range(8):
                    for mt in range(4):
                        o = (mt * 8 + k) * 128
                        mm(psum[:, zb[mt], :], sl[:, o:o + 128], hn[:, k, :], k == 0, k == 7, [rs, R_hn[k]], [R_ps[zb[mt]]])
                for mt in range(4):
                    copy_op("act", vpool[:, mt, 16:528], psum[:, zb[mt], :], [R_ps[zb[mt]]], [R_vpool])
                P.tag = "poolel"
                for gi, wdw in enumerate((2, 4, 8, 16)):
                    v = vpool[:, gi, :]
                    L = 528
                    rv = [R_vpool, R_ptmp] + R_gel
                    tt(cpe, tA[:, 1:L], v[:, 1:L], v[:, 0:L - 1], ALU.add, rv, R_ptmp_all)
                    cur = tA
                    if wdw >= 4:
                        tt(cpe, tB[:, 3:L], tA[:, 3:L], tA[:, 1:L - 2], ALU.add, rv, R_ptmp_all)
                        cur = tB
                    if wdw >= 8:
                        tt(cpe, tA[:, 7:L], tB[:, 7:L], tB[:, 3:L - 4], ALU.add, rv, R_ptmp_all)
                        cur = tA
                    if wdw >= 16:
                        tt(cpe, tB[:, 15:L], tA[:, 15:L], tA[:, 7:L - 8], ALU.add, rv, R_ptmp_all)
                        cur = tB
                    if first:
                        tt(cpe, cur[:, 16:32], cur[:, 16:32], pcorr[:, gi, :], ALU.mult, rv + [R_const], R_ptmp_all)
                    tt(cpe, tC[:, 0:512], v[:, 16:L], v[:, 16:L], ALU.add, rv, R_ptmp_all)
                    for _ in range(gi):
                        tt(cpe, tC[:, 0:512], tC[:, 0:512], tC[:, 0:512], ALU.add, rv, R_ptmp_all)
                    tt(cpe, pooled[:, gi, :], cur[:, 16:L], tC[:, 0:512], ALU.subtract, rv, [R_pooled])
                P.add(cpe, lambda e: e.tensor_copy(out=vpool[:, :, 0:16], in_=vpool[:, :, 512:528]), reads=[R_vpool, R_ptmp] + R_gel, writes=[R_vpool])
                P.tag = "epath"
                for hb in range(2):
                    b = bank()
                    for q in range(4):
                        idx = hb * 4 + q
                        bq, kc = idx // 2, idx % 2
                        tr(psum[:, b, q * 128:(q + 1) * 128], ptok[:, bq, kc * 128:(kc + 1) * 128], ident[:], [R_ptok, R_const], [R_ps[b]])
                    for q in range(4):
                        idx = hb * 4 + q
                        bq, kc = idx // 2, idx % 2
                        copy_op("act", pT[:, kc, bq * 128:(bq + 1) * 128], psum[:, b, q * 128:(q + 1) * 128], [R_ps[b]], [R_pT])
                sl, rs = wload("proj", first)
                for mt in range(8):
                    b = bank()
                    for k in range(2):
                        o = (mt * 2 + k) * 128
                        mm(psum[:, b, :], sl[:, o:o + 128], pT[:, k, :], k == 0, k == 1, [rs, R_pT], [R_ps[b]])
                    act(e_bf[:, mt, :], psum[:, b, :], AF.Copy, [R_ps[b], R_const], [R_ebf[mt]], scale=cols[:, C_PLE + mt:C_PLE + mt + 1])
                    act(sq[:, mt, :], psum[:, b, :], AF.Square, [R_ps[b]], [R_H[mt]])
                bst = bank()
                for mt in range(8):
                    mm(psum[:, bst, :], onesb[:], sq[:, mt, :], mt == 0, mt == 7, [R_H[mt], R_const], [R_ps[bst]])
                act(rstd_e[:], psum[:, bst, :], AF.Sqrt, [R_ps[bst]], [R_rstde], scale=1.0 / D, bias=EPS)
                warm(WY)
                P.tag = "Y"
                gel128 = CC[:, 4192:6240].rearrange("p (a l) -> p a l", a=4)
                Y2 = HB[:, 0:4, :].rearrange("p a b -> p (a b)").rearrange("p (t x) -> p t x", t=8)
                R_Y2n = R_H[0:4]
                v4 = "p (g t c) -> p g t c"
                for qq in range(4):
                    b = bank()
                    for ph in range(2):
                        q = 2 * qq + ph
                        for gg in range(4):
                            g = q * 4 + gg
                            half, gp = g // 16, g % 16
                            hs = slice(half * 64, half * 64 + 64)
                            o = psum[ph * 64:(ph + 1) * 64, b, gg * 128:(gg + 1) * 128]
                            rr = R_U + R_Sabf + [R_ops]
                            mm(o, U[:, g, :], RTt[:, g, :], True, False, rr, [R_ps[b]])
                            mm(o, Sabf[hs, gp, :], Qs[hs, gp, 0, :], False, False, rr, [R_ps[b]])
                            mm(o, Sabf[hs, 16 + gp, :], Qs[hs, gp, 1, :], False, True, rr, [R_ps[b]])
                    pv = psum[:, b, :]
                    st_ = qq % 2
                    g1, g2 = gel128[:, 2 * st_, :], gel128[:, 2 * st_ + 1, :]
                    rg = [R_gel[st_]]
                    act(g1, pv, AF.Square, [R_ps[b]] + rg, rg)
                    stt(g1, g1, 1.0 / 0.044715, pv, ALU.add, ALU.mult, [R_ps[b]] + rg, rg)
                    act(g2, g1, AF.Sigmoid, rg, rg, scale=1.5957691216057308 * 0.044715)
                    yo = Y2[:, :, qq * 64:(qq + 1) * 64].rearrange("p t (g c) -> p g t c", g=4)
                    tt("dve", yo, pv.rearrange(v4, g=4, t=8), g2.rearrange(v4, g=4, t=8), ALU.mult, [R_ps[b]] + rg, R_Y2n)
                P.tag = "yT"
                P.add("dve", lambda e: e.reciprocal(out=rstd_e[:], in_=rstd_e[:]), reads=[R_rstde], writes=[R_rstde])
                warm(WT)
                for kc in range(4):
                    b = bank()
                    for t in range(8):
                        for ph in range(2):
                            hp = slice(ph * 64, ph * 64 + 64)
                            tr(psb[hp, b, t * 64:(t + 1) * 64], Y2[hp, t, kc * 64:(kc + 1) * 64], identb[hp, hp],
                               R_Y2n + [R_const], [R_ps[b]])
                    copy_op("act", ysb[:, kc, :].rearrange("p (c t) -> p t c", t=8),
                            psb[:, b, 0:512].rearrange("p (t c) -> p t c", t=8), [R_ps[b]], [R_ysb[kc]])
                P.tag = "pooltail"
                class _V:
                    pass
                for gi in range(4):
                    b = bank()
                    mm(psum[:, b, :], slg[:, 2048 + gi * 128:2048 + (gi + 1) * 128], pooled[:, gi, :], True, True, [rsg, R_pooled], [R_ps[b]])
                    act(yT[:, 4 + gi, :], psum[:, b, :], AF.Copy, [R_ps[b], R_const], [R_yT[4 + gi]], scale=pscw[:, gi:gi + 1])
                    act(sq[:, 4 + gi, :], psum[:, b, :], AF.Square, [R_ps[b], R_const], [R_H[4 + gi]], scale=pscw[:, gi:gi + 1])
                stats_rstd(None, 4, None, 1.0 / 512, presq=(sq[:, 4:8, :], R_H[4:8]), outt=(tC[:], [R_ptmp]))
                for k in range(4):
                    stt(yT[:, 4 + k, :], yT[:, 4 + k, :], cols[:, C_POOLN + k:C_POOLN + k + 1], tC[:], ALU.mult, ALU.mult,
                        [R_yT[4 + k], R_ptmp, R_const], [R_yT[4 + k]])
                P.tag = "GLU"
                for mt in range(4):
                    b = bank()
                    for k in range(4):
                        o = (mt * 4 + k) * 128
                        mm(psum[:, b, :], slg[:, o:o + 128], ysb[:, k, :], k == 0, k == 3, [rsg, R_ysb[k]], [R_ps[b]])
                    gt_, Rg_ = [(gtmp[:], R_gtmp), (silu_t[:, 0, :], R_silu[0]), (silu_t[:, 1, :], R_silu[1])][mt % 3]
                    act(gt_, psum[:, b, :], AF.Sigmoid, [R_ps[b], R_const], [Rg_], bias=cols[:, C_BGLU + mt:C_BGLU + mt + 1])
                    tt("dve", yf[:, mt, :], ysb[:, mt, :], gt_, ALU.mult, [R_ysb[mt], Rg_], R_yf_all if mt == 0 else [R_yfm[mt]])
                    act(sq[:, mt, :], yf[:, mt, :], AF.Square, [R_yfm[mt]], [R_H[mt]])
                stats_rstd(None, 4, None, 1.0 / 512, presq=(sq, R_H))
                for k in range(4):
                    stt(yT[:, k, :], yf[:, k, :], cols[:, C_SSMN + k:C_SSMN + k + 1], rstd[:], ALU.mult, ALU.mult,
                        [R_yfm[k], R_rstd, R_const], [R_yT[k]])
                P.tag = "wout"
                warm(WO)
                for ob in range(2):
                    sl, rs = wload("wout%d" % ob, first)
                    for mt in range(4):
                        b = bank()
                        for k in range(8):
                            o = (mt * 8 + k) * 128
                            mm(psum[:, b, :], sl[:, o:o + 128], yT[:, k, :], k == 0, k == 7, [rs, R_yT[k]], [R_ps[b]])
                        m = ob * 4 + mt
                        tt("dve", hT[:, m, :], hT[:, m, :], psum[:, b, :], ALU.add, [R_ps[b], R_hT[m]], [R_hT[m]])
                        if dbg >= 3:
                            act(sqB[:, m, :], hT[:, m, :], AF.Square, [R_hT[m]], [R_sqB[m]])
            if dbg >= 3:
                P.tag = "F2"
                for m in range(8):
                    tt(cpe, e_bf[:, m, :], e_bf[:, m, :], rstd_e[:], ALU.mult, [R_ebf[m], R_rstde], [R_ebf[m]])
                norm_gain_only(C_FFN2, (sqB, R_sqB))
                ffn(2, first, (sqB, R_sqB) if dbg >= 3.1 else None)
            if dbg >= 3.1:
                P.tag = "PLE"
                norm_full(C_GATE, (sqB, R_sqB))
                warm(WG)
                for ob in range(2):
                    sl, rs = wload("gate%d" % ob, first)
                    gb = [bank() for _ in range(4)]
                    if ob == 0:
                        for k in range(8):
                            for mt in range(4):
                                o = (mt * 8 + k) * 128
                                mm(psum[:, gb[mt], :], sl[:, o:o + 128], hn[:, k, :], k == 0, k == 7, [rs, R_hn[k]], [R_ps[gb[mt]]])
                    else:
                        for mt in range(4):
                            for k in range(8):
                                o = (mt * 8 + k) * 128
                                mm(psum[:, gb[mt], :], sl[:, o:o + 128], hn[:, k, :], k == 0, k == 7, [rs, R_hn[k]], [R_ps[gb[mt]]])
                    for mt in range(4):
                        b = gb[mt]
                        m = ob * 4 + mt
                        gt_, Rg_ = [(gtmp[:], R_gtmp), (silu_t[:, 0, :], R_silu[0]), (silu_t[:, 1, :], R_silu[1])][m % 3]
                        act(gt_, psum[:, b, :], AF.Sigmoid, [R_ps[b]], [Rg_])
                        tt("dve", gt_, gt_, e_bf[:, m, :], ALU.mult, [Rg_, R_ebf[m]], [Rg_])
                        tt("dve", ebuf[:, m, :], hT[:, m, :], gt_, ALU.add, [Rg_, R_hT[m]], [R_B[m]])
                if w + 1 < n_win:
                    load_x_piece(w + 1, 0)
                    load_x_piece(w + 1, 1)
            P.tag = "fin"
            if dbg < 3.1:
                for k in range(8):
                    copy_op("act", ebuf[:, k, :], hT[:, k, :], [R_hT[k]], [R_B[k]])
            R_facc = Res("facc")
            for bq in range(4):
                bks = []
                for hb in range(2):
                    b = bank()
                    for kk in range(4):
                        k = hb * 4 + kk
                        tr(psum[:, b, kk * 128:(kk + 1) * 128], ebuf[:, k, bq * 128:(bq + 1) * 128], ident[:], [R_B[k], R_const], [R_ps[b]])
                    P.add("act", lambda e, b=b, j=2 * bq + hb: e.activation(out=gtmp[:], in_=psum[:, b, :], func=AF.Square,
                                                                           accum_out=facc[:, j:j + 1]),
                          reads=[R_ps[b]], writes=[R_gtmp, R_facc])
                    bks.append(b)
                c0 = 8 + bq
                tt("dve", facc[:, c0:c0 + 1], facc[:, 2 * bq:2 * bq + 1], facc[:, 2 * bq + 1:2 * bq + 2], ALU.add, [R_facc], [R_facc])
                act(facc[:, c0:c0 + 1], facc[:, c0:c0 + 1], AF.Sqrt, [R_facc], [R_facc], scale=1.0 / D, bias=EPS)
                P.add("dve", lambda e, c0=c0: e.reciprocal(out=facc[:, c0 + 4:c0 + 5], in_=facc[:, c0:c0 + 1]), reads=[R_facc], writes=[R_facc])
                for hb in range(2):
                    b = bks[hb]
                    stt(otok[:, bq % 2, hb * 512:(hb + 1) * 512], psum[:, b, :], facc[:, c0 + 4:c0 + 5], fnrow[:, hb * 512:(hb + 1) * 512],
                        ALU.mult, ALU.mult, [R_ps[b], R_facc, R_const], [R_otok[bq % 2]])
                t0 = t0w + bq * 128
                op = P.add("act", lambda e, t0=t0, bq=bq: e.dma_start(out=out[t0:t0 + 128, :], in_=otok[:, bq % 2, :]),
                           reads=[R_otok[bq % 2]], dma=True)
                final_ops.append(op)

        P.emit(final_waits=final_ops)
    return nc


def _host_layout(inputs):
    f32 = np.float32

    def colv(v):
        v = np.asarray(v, f32).reshape(-1)
        return np.ascontiguousarray(v.reshape(-1, 128).T)

    cols = np.zeros((128, 64), f32)
    cols[:, 0:8] = colv(inputs["ffn1_norm"][0])
    cols[:, 8:16] = colv(inputs["mix_norm"][0])
    cols[:, 16:24] = colv(inputs["ffn2_norm"][0])
    cols[:, 24:32] = colv(inputs["ple_gate_norm"][0])
    cols[:, 32:40] = colv(inputs["ple_norm"][0])
    cols[:, 40:48] = colv(inputs["final_norm"])
    cols[:, 48:52] = colv(inputs["ssm_out_norm"][0])
    cols[:, 52:56] = colv(inputs["pool_out_norm"][0])
    cols[:, 56:60] = colv(inputs["pool_scale"][0])
    cols[:, 60:64] = colv(inputs["b_glu"][0])
    dsk = np.asarray(inputs["d_skip"][0], f32).reshape(32, 16)
    dcol = np.ascontiguousarray(np.broadcast_to(dsk.T[None, :, :], (8, 16, 32)).reshape(128, 32))

    def gn(a):
        a = np.asarray(a, f32).reshape(2, 16, 64)
        return np.ascontiguousarray(a.transpose(0, 2, 1).reshape(128, 16))

    s5a = np.zeros((128, 3, 16), f32)
    s5a[:, 0] = gn(inputs["a_re"][0])
    s5a[:, 1] = gn(inputs["a_im"][0])
    ld = np.asarray(inputs["log_dt"][0], f32).reshape(2, 1, 16)
    s5a[:, 2] = np.broadcast_to(ld, (2, 64, 16)).reshape(128, 16)

    def gnb(a):
        a = np.asarray(a, f32).reshape(2, 16, 64, 16)
        return np.ascontiguousarray(a.transpose(0, 2, 1, 3).reshape(128, 16, 16))

    def gnc(a):
        a = np.asarray(a, f32).reshape(2, 16, 16, 64)
        return np.ascontiguousarray(a.transpose(0, 3, 1, 2).reshape(128, 16, 16))

    s5b = np.stack([gnb(inputs["b_re"][0]), gnb(inputs["b_im"][0])], axis=1)
    s5c = np.stack([gnc(inputs["c_re"][0]), gnc(inputs["c_im"][0])], axis=1)
    ident = np.eye(128, dtype=f32)
    sidx = np.arange(128) // 16
    cmask = (sidx[None, :] >= sidx[:, None]).astype(f32)
    pcorr = np.zeros((128, 4, 16), f32)
    for gi, wdw in enumerate((2, 4, 8, 16)):
        t = np.arange(16)
        pcorr[:, gi, :] = (wdw / np.minimum(t + 1, wdw)).astype(f32)[None, :]
    common = {
        "cols": cols, "dcol": dcol, "s5a": s5a, "s5b": np.ascontiguousarray(s5b), "s5c": np.ascontiguousarray(s5c),
        "ident": ident, "cmask": cmask, "pcorr": pcorr,
        "fnrow": np.ascontiguousarray(np.broadcast_to(np.asarray(inputs["final_norm"], f32).reshape(1, D), (128, D))),
    }
    wts = {
        "ffn1_w1": inputs["ffn1_w1"], "ffn1_w3": inputs["ffn1_w3"], "ffn1_w2": inputs["ffn1_w2"],
        "ffn2_w1": inputs["ffn2_w1"], "ffn2_w3": inputs["ffn2_w3"], "ffn2_w2": inputs["ffn2_w2"],
        "w_in": inputs["w_in"], "w_out": inputs["w_out"], "w_glu": inputs["w_glu"], "w_pool": inputs["w_pool"],
        "w_ple_gate": inputs["w_ple_gate"], "w_ple_proj": inputs["w_ple_proj"],
    }
    for nm, v in wts.items():
        common[nm] = np.ascontiguousarray(np.asarray(v[0], f32))
    return common


_NC_CACHE = {}


def kernel(**inputs):
    common = _host_layout(inputs)
    x = np.asarray(inputs["x"], np.float32)
    p = np.asarray(inputs["p"], np.float32)[0]
    in_maps = []
    for b in range(8):
        m = dict(common)
        m["x"] = np.ascontiguousarray(x[b])
        m["p"] = np.ascontiguousarray(p[b])
        in_maps.append(m)
    nc = build_program()
    res = run_bass_kernel_spmd(nc, in_maps, core_ids=list(range(8)))
    outs = [np.asarray(r["out"], np.float32).reshape(SEQ, D) for r in res.results]
    return np.stack(outs, axis=0)
```
